# Optimizing a Trainium2 kernel written in Bass

```python
import math
import jax, jax.numpy as jnp
from jax import lax
import numpy as np

D_MODEL = 4096
BATCH = 2
SEQ = 4096
DEPTH = 4

GRID_W = 64
CTX_LEN = 256
Q_BLOCK = 128
ROPE_DIM = 64
ROPE_QUARTER = ROPE_DIM // 4
ROPE_BASE = 10000.0
RMS_EPS = 1e-6

S5_WIDTH = D_MODEL // 4
S5_GROUP = 16
S5_GROUPS = S5_WIDTH // S5_GROUP
S5_STATE = 64
S5_DT_MIN = 1e-3
S5_DT_MAX = 1e-1

DIFF_HEADS = D_MODEL // 512
DIFF_QK = 64
DIFF_V = 2 * DIFF_QK
DIFF_WIDTH = DIFF_HEADS * DIFF_V

MLA_HEADS = D_MODEL // 256
MLA_Q_RANK = 3 * D_MODEL // 16
MLA_KV_RANK = D_MODEL // 8
MLA_NOPE = 128
MLA_ROPE = ROPE_DIM
MLA_V = 128
MLA_WIDTH = MLA_HEADS * MLA_V

N_BRANCH = 3
KV_SPLITS = (S5_WIDTH, DIFF_HEADS * 2 * DIFF_QK, DIFF_WIDTH, MLA_KV_RANK, MLA_ROPE)
Q_SPLITS = (DIFF_HEADS * 2 * DIFF_QK, MLA_Q_RANK, S5_WIDTH, DIFF_WIDTH, MLA_WIDTH, N_BRANCH * D_MODEL)
KV_COLS = sum(KV_SPLITS)
IN_COLS = KV_COLS + sum(Q_SPLITS)

kernel_name = 'hybrid_s5_diffattn_mla_prefix_dit'


def rms_norm(x, g):
    xf = x.astype(jnp.float32)
    y = xf * lax.rsqrt(jnp.mean(xf * xf, axis=-1, keepdims=True) + RMS_EPS)
    return y.astype(x.dtype) * g


def split_cols(p, sizes):
    return jnp.split(p, np.cumsum(sizes)[:-1].tolist(), axis=-1)


def axial_rope_tables(rows, dtype):
    t = jnp.arange(rows * GRID_W)
    pos = jnp.stack([t // GRID_W, t % GRID_W], axis=-1).astype(jnp.float32)
    inv_freq = ROPE_BASE ** (-jnp.arange(ROPE_QUARTER, dtype=jnp.float32) / ROPE_QUARTER)
    ang = pos[:, :, None] * inv_freq
    return jnp.cos(ang).astype(dtype), jnp.sin(ang).astype(dtype)


def apply_rope(x, cos, sin):
    shp = x.shape
    xr = x.reshape(shp[:-1] + (2, 2, ROPE_QUARTER))
    x1, x2 = xr[..., 0, :], xr[..., 1, :]
    bshape = (shp[1],) + (1,) * (x.ndim - 3) + (2, ROPE_QUARTER)
    cs, sn = cos.reshape(bshape), sin.reshape(bshape)
    return jnp.stack([x1 * cs - x2 * sn, x1 * sn + x2 * cs], axis=-2).reshape(shp)


def query_blocks(fn, qs):
    b, n = qs[0].shape[:2]
    nb = n // Q_BLOCK
    qb = tuple(jnp.moveaxis(q.reshape((b, nb, Q_BLOCK) + q.shape[2:]), 1, 0) for q in qs)
    out = lax.map(lambda blk: fn(*blk), qb)
    out = jnp.moveaxis(out, 0, 1)
    return out.reshape((b, n) + out.shape[3:])


def s5_discretize(lam_re, lam_im, log_dt, b_re, b_im):
    dt = jnp.exp(log_dt)[:, None]
    lr = jnp.minimum(lam_re, -1e-4)
    er = jnp.exp(lr * dt)
    a_re, a_im = er * jnp.cos(lam_im * dt), er * jnp.sin(lam_im * dt)
    nr, ni = a_re - 1.0, a_im
    den = lr * lr + lam_im * lam_im
    fr, fi = (nr * lr + ni * lam_im) / den, (ni * lr - nr * lam_im) / den
    bb_re = fr[..., None] * b_re - fi[..., None] * b_im
    bb_im = fr[..., None] * b_im + fi[..., None] * b_re
    return a_re, a_im, bb_re, bb_im


def _ssm_combine(e1, e2):
    a1r, a1i, b1r, b1i = e1
    a2r, a2i, b2r, b2i = e2
    return (a1r * a2r - a1i * a2i, a1r * a2i + a1i * a2r,
            a2r * b1r - a2i * b1i + b2r, a2r * b1i + a2i * b1r + b2i)


def s5_scan(u, a_re, a_im, bb_re, bb_im, h0, reverse):
    bu_re = jnp.einsum('blgh,gph->blgp', u, bb_re)
    bu_im = jnp.einsum('blgh,gph->blgp', u, bb_im)
    if h0 is not None:
        h_re, h_im = h0
        idx = -1 if reverse else 0
        bu_re = bu_re.at[:, idx].add(a_re * h_re - a_im * h_im)
        bu_im = bu_im.at[:, idx].add(a_re * h_im + a_im * h_re)
    seq = u.shape[1]
    ar = jnp.broadcast_to(a_re, (1, seq) + a_re.shape)
    ai = jnp.broadcast_to(a_im, (1, seq) + a_im.shape)
    _, _, x_re, x_im = lax.associative_scan(_ssm_combine, (ar, ai, bu_re, bu_im), reverse=reverse, axis=1)
    return x_re, x_im


def s5_readout(x_re, x_im, c_re, c_im):
    return jnp.einsum('blgp,ghp->blgh', x_re, c_re) - jnp.einsum('blgp,ghp->blgh', x_im, c_im)


def s5_mixer(u_l, u_c, lp, need_ctx):
    ul = u_l.reshape(u_l.shape[:2] + (S5_GROUPS, S5_GROUP))
    uc = u_c.reshape(u_c.shape[:2] + (S5_GROUPS, S5_GROUP))
    y_l = lp['s5_d'] * ul
    y_c = lp['s5_d'] * uc if need_ctx else None
    for di, rev in enumerate((False, True)):
        a_re, a_im, bb_re, bb_im = s5_discretize(lp['s5_lambda_re'][di], lp['s5_lambda_im'][di],
                                                 lp['s5_log_dt'][di], lp['s5_b_re'][di], lp['s5_b_im'][di])
        xc_re, xc_im = s5_scan(uc, a_re, a_im, bb_re, bb_im, None, rev)
        end = 0 if rev else -1
        xl_re, xl_im = s5_scan(ul, a_re, a_im, bb_re, bb_im, (xc_re[:, end], xc_im[:, end]), rev)
        y_l = y_l + s5_readout(xl_re, xl_im, lp['s5_c_re'][di], lp['s5_c_im'][di])
        if need_ctx:
            y_c = y_c + s5_readout(xc_re, xc_im, lp['s5_c_re'][di], lp['s5_c_im'][di])
    return y_l.reshape(u_l.shape), (y_c.reshape(u_c.shape) if need_ctx else None)


def s5_finish(y, g, lp):
    z = jax.nn.gelu(y)
    z = z * jax.nn.sigmoid(z @ lp['w_glu'])
    return (z * jax.nn.silu(g)) @ lp['w_br_s5']


def diff_heads_qk(t):
    return t.reshape(t.shape[:2] + (DIFF_HEADS, 2, DIFF_QK))


def diff_heads_v(t):
    return t.reshape(t.shape[:2] + (DIFF_HEADS, DIFF_V))


def diff_attn_core(q, k, v, lam):
    s = jnp.einsum('bqhmd,bkhmd->bhmqk', q, k).astype(jnp.float32) * (DIFF_QK ** -0.5)
    p = jax.nn.softmax(s, axis=-1)
    w = (p[:, :, 0] - lam * p[:, :, 1]).astype(v.dtype)
    return jnp.einsum('bhqk,bkhd->bqhd', w, v)


def diff_finish(o, g, lp, lam_init):
    o = rms_norm(o, lp['g_diff']) * (1.0 - lam_init)
    o = o.reshape(o.shape[:2] + (DIFF_WIDTH,))
    return (o * jax.nn.silu(g)) @ lp['w_br_diff']


def mla_kv(ckv, lp):
    kv = rms_norm(ckv, lp['g_kv']) @ lp['w_ukv']
    kv = kv.reshape(kv.shape[:2] + (MLA_HEADS, MLA_NOPE + MLA_V))
    return kv[..., :MLA_NOPE], kv[..., MLA_NOPE:]


def mla_q(cq, lp):
    q = rms_norm(cq, lp['g_q']) @ lp['w_uq']
    q = q.reshape(q.shape[:2] + (MLA_HEADS, MLA_NOPE + MLA_ROPE))
    return q[..., :MLA_NOPE], q[..., MLA_NOPE:]


def mla_core(q_nope, q_rope, k_nope, k_rope, v):
    s = jnp.einsum('bqhd,bkhd->bhqk', q_nope, k_nope) + jnp.einsum('bqhd,bkd->bhqk', q_rope, k_rope)
    p = jax.nn.softmax(s.astype(jnp.float32) * ((MLA_NOPE + MLA_ROPE) ** -0.5), axis=-1)
    return jnp.einsum('bhqk,bkhd->bqhd', p.astype(v.dtype), v)


def mla_finish(o, g, lp):
    o = o.reshape(o.shape[:2] + (MLA_WIDTH,))
    return (o * jax.nn.silu(g)) @ lp['w_br_mla']


def merge_branches(ya, yb, yc, gm, lp):
    g_a, g_b, g_c = jnp.split(gm, N_BRANCH, axis=-1)
    y = jax.nn.sigmoid(g_a) * ya + jax.nn.sigmoid(g_b) * yb + jax.nn.sigmoid(g_c) * yc
    return rms_norm(y @ lp['w_out'], lp['g_post'])


def hybrid_layer(x, ctx, c, c_ctx, lp, layer_idx, cos, sin, need_ctx):
    d = D_MODEL
    mod = jax.nn.silu(c) @ lp['w_mod'] + lp['b_mod']
    n_mod_c = (3 if need_ctx else 2) * d
    mod_c = jax.nn.silu(c_ctx) @ lp['w_mod'][:, :n_mod_c] + lp['b_mod'][:n_mod_c]
    h = rms_norm(x, lp['g_pre']) * (1.0 + mod[:, None, d:2 * d]) + mod[:, None, :d]
    hc = rms_norm(ctx, lp['g_pre']) * (1.0 + mod_c[d:2 * d]) + mod_c[:d]

    p_l = h @ lp['w_in']
    p_c = hc @ (lp['w_in'] if need_ctx else lp['w_in'][:, :KV_COLS])
    u_l, dk_l, dv_l, ckv_l, kr_l = split_cols(p_l[..., :KV_COLS], KV_SPLITS)
    dq_l, cq_l, ga_l, gb_l, gc_l, gm_l = split_cols(p_l[..., KV_COLS:], Q_SPLITS)
    u_c, dk_c, dv_c, ckv_c, kr_c = split_cols(p_c[..., :KV_COLS], KV_SPLITS)

    ya_l, ya_c = s5_mixer(u_l, u_c, lp, need_ctx)

    lam_init = 0.8 - 0.6 * math.exp(-0.3 * layer_idx)
    dl = lp['diff_lambda'].astype(jnp.float32)
    lam = jnp.exp(jnp.sum(dl[0] * dl[1])) - jnp.exp(jnp.sum(dl[2] * dl[3])) + lam_init
    dkc, dvc = diff_heads_qk(dk_c), diff_heads_v(dv_c)
    dk_all = jnp.concatenate([dkc, apply_rope(diff_heads_qk(dk_l), cos, sin)], axis=1)
    dv_all = jnp.concatenate([dvc, diff_heads_v(dv_l)], axis=1)
    dq = apply_rope(diff_heads_qk(dq_l), cos, sin)
    ob_l = query_blocks(lambda q: diff_attn_core(q, dk_all, dv_all, lam), (dq,))

    kn_c, mv_c = mla_kv(ckv_c, lp)
    kn_l, mv_l = mla_kv(ckv_l, lp)
    kn_all = jnp.concatenate([kn_c, kn_l], axis=1)
    mv_all = jnp.concatenate([mv_c, mv_l], axis=1)
    kr_all = jnp.concatenate([kr_c, apply_rope(kr_l, cos, sin)], axis=1)
    qn_l, qr_l = mla_q(cq_l, lp)
    oc_l = query_blocks(lambda qn, qr: mla_core(qn, qr, kn_all, kr_all, mv_all),
                        (qn_l, apply_rope(qr_l, cos, sin)))

    y_l = merge_branches(s5_finish(ya_l, ga_l, lp), diff_finish(ob_l, gb_l, lp, lam_init),
                         mla_finish(oc_l, gc_l, lp), gm_l, lp)
    x_out = x + mod[:, None, 2 * d:] * y_l
    if not need_ctx:
        return x_out, ctx

    dq_c, cq_c, ga_c, gb_c, gc_c, gm_c = split_cols(p_c[..., KV_COLS:], Q_SPLITS)
    ob_c = diff_attn_core(diff_heads_qk(dq_c), dkc, dvc, lam)
    qn_c, qr_c = mla_q(cq_c, lp)
    oc_c = mla_core(qn_c, qr_c, kn_c, kr_c, mv_c)
    y_c = merge_branches(s5_finish(ya_c, ga_c, lp), diff_finish(ob_c, gb_c, lp, lam_init),
                         mla_finish(oc_c, gc_c, lp), gm_c, lp)
    ctx_out = ctx + mod_c[2 * d:] * y_c
    return x_out, ctx_out


def setup_inputs(seed: int = 0) -> dict:
    key = jax.random.key(seed)
    ks = iter(jax.random.split(key, 40))
    f32 = jnp.float32

    def nrm(shape, scale):
        return jax.random.normal(next(ks), shape, f32) * scale

    def gain(shape):
        return 1.0 + nrm(shape, 0.02)

    L, D = DEPTH, D_MODEL
    G, P, H = S5_GROUPS, S5_STATE, S5_GROUP
    x = nrm((BATCH, SEQ, D), 1.0)
    c = nrm((BATCH, D), 1.0)
    ctx = nrm((BATCH, CTX_LEN, D), 1.0)
    c_ctx = nrm((D,), 1.0)
    w_mod = nrm((L, D, 3 * D), 0.5 * D ** -0.5)
    b_mod = nrm((L, 3 * D), 0.01)
    g_pre = gain((L, D))
    g_post = gain((L, D))
    w_in = nrm((L, D, IN_COLS), D ** -0.5)
    s5_lambda_re = -0.5 + nrm((L, 2, G, P), 0.01)
    s5_lambda_im = math.pi * jnp.arange(P, dtype=f32) + nrm((L, 2, G, P), 0.01)
    s5_log_dt = jax.random.uniform(next(ks), (L, 2, G), f32, math.log(S5_DT_MIN), math.log(S5_DT_MAX))
    s5_b_re = nrm((L, 2, G, P, H), (2 * H) ** -0.5)
    s5_b_im = nrm((L, 2, G, P, H), (2 * H) ** -0.5)
    s5_c_re = nrm((L, 2, G, H, P), P ** -0.5)
    s5_c_im = nrm((L, 2, G, H, P), P ** -0.5)
    s5_d = nrm((L, G, H), 1.0)
    w_glu = nrm((L, S5_WIDTH, S5_WIDTH), S5_WIDTH ** -0.5)
    diff_lambda = nrm((L, 4, DIFF_QK), 0.1)
    g_diff = gain((L, DIFF_V))
    w_uq = nrm((L, MLA_Q_RANK, MLA_HEADS * (MLA_NOPE + MLA_ROPE)), MLA_Q_RANK ** -0.5)
    g_q = gain((L, MLA_Q_RANK))
    w_ukv = nrm((L, MLA_KV_RANK, MLA_HEADS * (MLA_NOPE + MLA_V)), MLA_KV_RANK ** -0.5)
    g_kv = gain((L, MLA_KV_RANK))
    w_br_s5 = nrm((L, S5_WIDTH, D), S5_WIDTH ** -0.5)
    w_br_diff = nrm((L, DIFF_WIDTH, D), DIFF_WIDTH ** -0.5)
    w_br_mla = nrm((L, MLA_WIDTH, D), MLA_WIDTH ** -0.5)
    w_out = nrm((L, D, D), D ** -0.5)
    return {'x': x, 'c': c, 'ctx': ctx, 'c_ctx': c_ctx, 'w_mod': w_mod, 'b_mod': b_mod,
            'g_pre': g_pre, 'g_post': g_post, 'w_in': w_in,
            's5_lambda_re': s5_lambda_re, 's5_lambda_im': s5_lambda_im, 's5_log_dt': s5_log_dt,
            's5_b_re': s5_b_re, 's5_b_im': s5_b_im, 's5_c_re': s5_c_re, 's5_c_im': s5_c_im,
            's5_d': s5_d, 'w_glu': w_glu, 'diff_lambda': diff_lambda, 'g_diff': g_diff,
            'w_uq': w_uq, 'g_q': g_q, 'w_ukv': w_ukv, 'g_kv': g_kv,
            'w_br_s5': w_br_s5, 'w_br_diff': w_br_diff, 'w_br_mla': w_br_mla, 'w_out': w_out}


def reference(x, c, ctx, c_ctx, w_mod, b_mod, g_pre, g_post, w_in,
              s5_lambda_re, s5_lambda_im, s5_log_dt, s5_b_re, s5_b_im, s5_c_re, s5_c_im,
              s5_d, w_glu, diff_lambda, g_diff, w_uq, g_q, w_ukv, g_kv,
              w_br_s5, w_br_diff, w_br_mla, w_out):
    rows = x.shape[1] // GRID_W
    cos, sin = axial_rope_tables(rows, x.dtype)
    for l in range(DEPTH):
        lp = {'w_mod': w_mod[l], 'b_mod': b_mod[l], 'g_pre': g_pre[l], 'g_post': g_post[l],
              'w_in': w_in[l], 's5_lambda_re': s5_lambda_re[l], 's5_lambda_im': s5_lambda_im[l],
              's5_log_dt': s5_log_dt[l], 's5_b_re': s5_b_re[l], 's5_b_im': s5_b_im[l],
              's5_c_re': s5_c_re[l], 's5_c_im': s5_c_im[l], 's5_d': s5_d[l], 'w_glu': w_glu[l],
              'diff_lambda': diff_lambda[l], 'g_diff': g_diff[l], 'w_uq': w_uq[l], 'g_q': g_q[l],
              'w_ukv': w_ukv[l], 'g_kv': g_kv[l], 'w_br_s5': w_br_s5[l], 'w_br_diff': w_br_diff[l],
              'w_br_mla': w_br_mla[l], 'w_out': w_out[l]}
        x, ctx = hybrid_layer(x, ctx, c, c_ctx, lp, l, cos, sin, l < DEPTH - 1)
    return x
```

```python
import math
import numpy as np
import ml_dtypes
from contextlib import ExitStack
import concourse.bass as bass
import concourse.mybir as mybir
from concourse.bass_utils import run_bass_kernel_spmd

F32 = mybir.dt.float32
BF16 = mybir.dt.bfloat16
AF = mybir.ActivationFunctionType
ALU = mybir.AluOpType
AX = mybir.AxisListType
NPBF = ml_dtypes.bfloat16

D = 4096
DEPTH = 4
NB = 2
SEQ = 4096
LC = 256
NTOK = SEQ + LC
KV_COLS = 3648
IN_COLS = 21824
EPS = 1e-6
SAME_ENGINE_WINDOW = 6


class Buf:
    __slots__ = ("t", "w", "r", "dsem", "dcnt", "name")

    def __init__(self, t, name=""):
        self.t = t
        self.w = None
        self.r = []
        self.dsem = None
        self.dcnt = 0
        self.name = name

    def __getitem__(self, idx):
        return self.t[idx]


class KB:
    def __init__(self):
        self.nc = bass.Bass("TRN2", target_bir_lowering=False)
        nc = self.nc
        self.es = ExitStack()
        self.eng = {"pe": nc.tensor, "act": nc.scalar, "dve": nc.vector, "pool": nc.gpsimd, "sp": nc.sync}
        self.clk = {}
        self.cnt = {}
        for e in ("pe", "act", "dve", "pool"):
            self.clk[e] = self.es.enter_context(nc.semaphore("clk_" + e))
            self.cnt[e] = 0
        self.waited = {e: {} for e in self.eng}
        self.dma_toks = {}
        self.n = 0
        self.psums = []
        self.ps_i = 0
        self.sem_free = []
        self.phase_bufs = []

    def uname(self, p):
        self.n += 1
        return "%s_%d" % (p, self.n)

    def sbuf(self, shape, dt, name="sb", es=None):
        t = (es or self.es).enter_context(self.nc.sbuf_tensor(self.uname(name), list(shape), dt))
        b = Buf(t, name)
        if es is not None:
            self.phase_bufs.append(b)
        return b

    def psum_pool(self, nbanks=8):
        for i in range(nbanks):
            t = self.es.enter_context(self.nc.psum_tensor(self.uname("ps"), [128, 512], F32))
            self.psums.append(Buf(t, "ps%d" % i))

    def ps(self):
        b = self.psums[self.ps_i % len(self.psums)]
        self.ps_i += 1
        return b

    def dram(self, name, shape, dt, kind):
        return Buf(self.nc.dram_tensor(name, list(shape), dt, kind=kind).ap(), name)

    def ensure(self, e, tok):
        if tok is None:
            return
        sem, val, src = tok
        if src == e:
            if e == "pe" or self.cnt[e] - val >= SAME_ENGINE_WINDOW:
                return
        k = id(sem)
        if self.waited[e].get(k, 0) >= val:
            return
        self.waited[e][k] = val
        self.eng[e].wait_ge(sem, val)

    def deps(self, e, reads, writes):
        for b in reads:
            self.ensure(e, b.w)
        for b in writes:
            self.ensure(e, b.w)
            for t in b.r:
                self.ensure(e, t)

    def op(self, e, fn, reads=(), writes=()):
        self.deps(e, reads, writes)
        ins = fn(self.eng[e])
        self.cnt[e] += 1
        ins.then_inc(self.clk[e], 1)
        tok = (self.clk[e], self.cnt[e], e)
        for b in writes:
            b.w = tok
            b.r = []
        for b in reads:
            if b.w is not tok:
                b.r.append(tok)
                if len(b.r) > 24:
                    b.r = self._compact(b.r)
        return tok

    @staticmethod
    def _compact(toks):
        best = {}
        for sem, val, src in toks:
            k = id(sem)
            if k not in best or best[k][1] < val:
                best[k] = (sem, val, src)
        return list(best.values())

    def dma(self, q, out_b, out_ap, in_b, in_ap, nowaw=False):
        if nowaw:
            self.deps(q, [in_b], [])
        else:
            self.deps(q, [in_b], [out_b])
        if out_b.dsem is None:
            if self.sem_free:
                out_b.dsem, out_b.dcnt = self.sem_free.pop()
            else:
                out_b.dsem = self.es.enter_context(self.nc.semaphore(self.uname("d")))
        ins = self.eng[q].dma_start(out=out_ap, in_=in_ap)
        out_b.dcnt += 16
        ins.then_inc(out_b.dsem, 16)
        tok = (out_b.dsem, out_b.dcnt, None)
        out_b.w = tok
        out_b.r = []
        in_b.r.append(tok)
        if len(in_b.r) > 24:
            in_b.r = self._compact(in_b.r)
        self.dma_toks[id(out_b.dsem)] = tok
        return tok

    def barrier(self):
        for e in self.eng:
            for f in self.clk:
                if f != e and self.cnt[f] > 0:
                    self.ensure(e, (self.clk[f], self.cnt[f], f))
            for tok in self.dma_toks.values():
                self.ensure(e, tok)

    def end_phase(self):
        self.barrier()
        for b in self.phase_bufs:
            if b.dsem is not None:
                self.sem_free.append((b.dsem, b.dcnt))
                b.dsem = None
        self.phase_bufs = []

    def finish(self):
        self.barrier()
        self.es.close()
        return self.nc


def run_spmd(nc, in_maps):
    res = run_bass_kernel_spmd(nc, in_maps, core_ids=list(range(len(in_maps))))
    return res.results


T = NTOK
TGS = [(0, 256, 1)] + [(256 + 512 * i, 512, 0) for i in range(8)]
NQC = 142
QC_DQ, QC_CQ, QC_GA, QC_GB, QC_GC, QC_GM = 0, 8, 14, 22, 30, 46


class Ring:
    def __init__(self, kb, shape, dt, n, es, name):
        self.bufs = [kb.sbuf(shape, dt, name, es) for _ in range(n)]
        self.i = 0

    def next(self):
        b = self.bufs[self.i % len(self.bufs)]
        self.i += 1
        return b


SHARED_INPUTS = ("pmat", "identb", "onesb", "jmat", "ropeC", "ropeS", "cvec")


class Layer:
    def __init__(self, dbg=(), phases=None, prev=None, lidx=0, last=True):
        self.dbg = set(dbg)
        self.phases = phases
        self.lidx = lidx
        self.last = last
        self.prev = prev
        if prev is None:
            self.kb = KB()
            self.kb.psum_pool(8)
            self.shared = {}
        else:
            self.kb = prev.kb
            self.shared = prev.shared
        self.evi = 0

    def din(self, name, shape, dt=F32):
        if name in SHARED_INPUTS:
            if name not in self.shared:
                self.shared[name] = self.kb.dram(name, shape, dt, "ExternalInput")
            return self.shared[name]
        if name == "xT":
            if self.prev is not None:
                return self.prev.xT_out
            return self.kb.dram(name, shape, dt, "ExternalInput")
        if self.prev is not None or not self.last:
            name = "%s_L%d" % (name, self.lidx)
        return self.kb.dram(name, shape, dt, "ExternalInput")

    def scratch(self, name, shape, dt=BF16):
        if name in self.shared:
            return self.shared[name]
        kind = "ExternalOutput" if name in self.dbg else "Internal"
        b = self.kb.dram(name, shape, dt, kind)
        self.shared[name] = b
        return b

    def on(self, p):
        return self.phases is None or p in self.phases

    def copy_evac(self, out_ap, in_ap, reads, writes):
        self.evi += 1
        if self.evi % 2:
            return self.kb.op("act", lambda e: e.activation(out=out_ap, in_=in_ap, func=AF.Copy), reads, writes)
        return self.kb.op("dve", lambda e: e.tensor_copy(out_ap, in_ap), reads, writes)

    def consts(self):
        kb = self.kb
        if self.prev is not None:
            for k in ("pmat", "ident", "ones", "jmat", "eps"):
                setattr(self, k, getattr(self.prev, k))
            return
        self.pmat_d = self.din("pmat", [128, 128], BF16)
        self.ident_d = self.din("identb", [128, 128], BF16)
        self.ones_d = self.din("onesb", [128, 128], BF16)
        self.jmat_d = self.din("jmat", [128, 128], BF16)
        self.pmat = kb.sbuf([128, 128], BF16, "pmat")
        self.ident = kb.sbuf([128, 128], BF16, "ident")
        self.ones = kb.sbuf([128, 128], BF16, "ones")
        self.jmat = kb.sbuf([128, 128], BF16, "jmat")
        for s, d in ((self.pmat, self.pmat_d), (self.ident, self.ident_d), (self.ones, self.ones_d),
                     (self.jmat, self.jmat_d)):
            kb.dma("sp", s, s[:], d, d[:, :])
        self.eps = kb.sbuf([128, 1], F32, "eps")
        kb.op("dve", lambda e: e.memset(self.eps[:], EPS), [], [self.eps])

    def p_mod(self):
        kb = self.kb
        cvec = self.din("cvec", [128, 32, 2])
        wmod = self.din("w_mod", [D, 3 * D])
        bmod = self.din("bmod", [128, 96])
        gpre = self.din("gpre", [128, 32])
        gpost = self.din("gpost", [128, 32])
        self.shift_s = kb.sbuf([128, 32, 2], F32, "shift")
        self.gs_s = kb.sbuf([128, 32, 2], F32, "gs")
        self.gpg_s = kb.sbuf([128, 32, 2], F32, "gpg")
        with ExitStack() as ph:
            cv_f = kb.sbuf([128, 32, 2], F32, "cvf", ph)
            cv_b = kb.sbuf([128, 32, 2], BF16, "cvb", ph)
            bm = kb.sbuf([128, 96], F32, "bm", ph)
            gp = kb.sbuf([128, 32], F32, "gp", ph)
            gq = kb.sbuf([128, 32], F32, "gq", ph)
            kb.dma("sp", cv_f, cv_f[:], cvec, cvec[:, :, :])
            kb.dma("sp", bm, bm[:], bmod, bmod[:, :])
            kb.dma("sp", gp, gp[:], gpre, gpre[:, :])
            kb.dma("sp", gq, gq[:], gpost, gpost[:, :])
            kb.op("act", lambda e: e.activation(out=cv_b[:], in_=cv_f[:], func=AF.Silu), [cv_f], [cv_b])
            ring = Ring(kb, [128, 32, 512], BF16, 2, ph, "wmod")
            wv = wmod.t.rearrange("(kc p) n -> p kc n", p=128)
            ws = {}

            def load(g):
                w = ring.next()
                kb.dma("pool", w, w[:], wmod, wv[:, :, g * 512:(g + 1) * 512])
                ws[g] = w
            load(0)
            for g in range(24):
                if g + 1 < 24:
                    load(g + 1)
                w = ws.pop(g)
                for q in range(4):
                    j = g * 4 + q
                    s, kf = j // 32, j % 32
                    ps = kb.ps()
                    for kc in range(32):
                        kb.op("pe", lambda e, kc=kc: e.matmul(ps[:, 0:2], w[:, kc, q * 128:(q + 1) * 128],
                                                             cv_b[:, kc, :], start=(kc == 0), stop=(kc == 31)),
                              [w, cv_b], [ps])
                    if s == 0:
                        kb.op("dve", lambda e: e.tensor_scalar(self.shift_s[:, kf, :], ps[:, 0:2], bm[:, j:j + 1],
                                                               None, ALU.add), [ps, bm], [self.shift_s])
                    elif s == 1:
                        kb.op("dve", lambda e: e.tensor_scalar(self.gs_s[:, kf, :], ps[:, 0:2], bm[:, j:j + 1], 1.0,
                                                               ALU.add, ALU.add), [ps, bm], [self.gs_s])
                        kb.op("dve", lambda e: e.tensor_scalar(self.gs_s[:, kf, :], self.gs_s[:, kf, :],
                                                               gp[:, kf:kf + 1], None, ALU.mult),
                              [gp], [self.gs_s])
                    else:
                        kb.op("dve", lambda e: e.tensor_scalar(self.gpg_s[:, kf, :], ps[:, 0:2], bm[:, j:j + 1],
                                                               gq[:, kf:kf + 1], ALU.add, ALU.mult),
                              [ps, bm, gq], [self.gpg_s])
            kb.end_phase()

    def p_prenorm(self):
        kb = self.kb
        self.xT = self.din("xT", [D, T])
        self.HT = self.scratch("HT", [32, 128, T])
        xv = self.xT.t.rearrange("(kc p) t -> p kc t", p=128)
        hv = self.HT.t.rearrange("c p t -> p c t")
        with ExitStack() as ph:
            xr = Ring(kb, [128, 32, 512], F32, 2, ph, "xb")
            hr = Ring(kb, [128, 32, 512], BF16, 2, ph, "hb")
            sqr = Ring(kb, [128, 512], BF16, 3, ph, "sq")
            tr = Ring(kb, [128, 512], F32, 3, ph, "t32")
            rstd = kb.sbuf([128, 512], F32, "rstd", ph)
            xs = {}

            def load(i):
                t0, n, v = TGS[i]
                b = xr.next()
                kb.dma("sp", b, b[:, :, 0:n], self.xT, xv[:, :, t0:t0 + n])
                xs[i] = b
            load(0)
            for i, (t0, n, v) in enumerate(TGS):
                if i + 1 < len(TGS):
                    load(i + 1)
                xb = xs.pop(i)
                ss = kb.ps()
                for kc in range(32):
                    sq = sqr.next()
                    kb.op("act", lambda e: e.activation(out=sq[:, 0:n], in_=xb[:, kc, 0:n], func=AF.Square),
                          [xb], [sq])
                    kb.op("pe", lambda e: e.matmul(ss[:, 0:n], self.ones[:], sq[:, 0:n], start=(kc == 0),
                                                   stop=(kc == 31)), [self.ones, sq], [ss])
                kb.op("act", lambda e: e.activation(out=rstd[:, 0:n], in_=ss[:, 0:n], func=AF.Sqrt,
                                                    scale=1.0 / D, bias=self.eps[:]), [ss, self.eps], [rstd])
                kb.op("dve", lambda e: e.reciprocal(rstd[:, 0:n], rstd[:, 0:n]), [], [rstd])
                hb = hr.next()
                for kc in range(32):
                    t32 = tr.next()
                    kb.op("dve", lambda e: e.scalar_tensor_tensor(t32[:, 0:n], xb[:, kc, 0:n],
                                                                  self.gs_s[:, kc, v:v + 1], rstd[:, 0:n],
                                                                  ALU.mult, ALU.mult),
                          [xb, self.gs_s, rstd], [t32])
                    kb.op("act", lambda e: e.activation(out=hb[:, kc, 0:n], in_=t32[:, 0:n], func=AF.Identity,
                                                        bias=self.shift_s[:, kc, v:v + 1], scale=1.0),
                          [t32, self.shift_s], [hb])
                kb.dma("sp", self.HT, hv[:, :, t0:t0 + n], hb, hb[:, :, 0:n], nowaw=True)
            kb.end_phase()


def pp_layout(v):
    return np.ascontiguousarray(np.asarray(v).reshape(-1, 128).T)


def host_consts():
    idx = np.arange(128)
    pmat = np.zeros((128, 128), np.float32)
    pmat[idx ^ 16, idx] = 1.0
    jm = np.zeros((128, 128), np.float32)
    jm[idx, 127 - idx] = 1.0
    r = np.arange(128)
    d = r % 64
    a, b, f = d // 32, (d // 16) % 2, d % 16
    inv = (np.float32(10000.0) ** (-np.arange(16, dtype=np.float32) / np.float32(16))).astype(np.float32)
    pos = np.arange(SEQ)
    prow, pcol = (pos // 64).astype(np.float32), (pos % 64).astype(np.float32)
    p2 = np.where(a[:, None] == 0, prow[None, :], pcol[None, :]).astype(np.float32)
    ang = (p2 * inv[f][:, None]).astype(np.float32)
    C = np.ones((128, T), np.float32)
    S = np.zeros((128, T), np.float32)
    C[:, LC:] = np.cos(ang)
    S[:, LC:] = np.sin(ang) * np.where(b == 0, -1.0, 1.0)[:, None].astype(np.float32)
    return {"pmat": pmat.astype(NPBF), "identb": np.eye(128, dtype=np.float32).astype(NPBF),
            "onesb": np.ones((128, 128), np.float32).astype(NPBF), "jmat": jm.astype(NPBF),
            "ropeC": C, "ropeS": S}


def _layer_p1(self):
    kb = self.kb
    self.w_in = self.din("w_in", [D, IN_COLS])
    ropeC_d = self.din("ropeC", [128, T])
    ropeS_d = self.din("ropeS", [128, T])
    self.kb_in = {"ropeC": ropeC_d, "ropeS": ropeS_d}
    self.U = self.scratch("U", [T, 1024])
    self.DV = self.scratch("DV", [T, 1024])
    self.DK = self.scratch("DK", [8, 128, T])
    self.CKV = self.scratch("CKV", [4, 128, T])
    self.KR = self.scratch("KR", [64, T])
    self.Q = self.scratch("Q", [NQC, 128, T])
    wv = self.w_in.t.rearrange("(kc p) n -> p kc n", p=128)
    hv = self.HT.t.rearrange("c p t -> p c t")
    groups = [("u", 0, 512), ("u", 512, 512), ("dk", 1024, 512), ("dk", 1536, 512), ("dv", 2048, 512),
              ("dv", 2560, 512), ("ckv", 3072, 512), ("kr", 3584, 64)]
    groups += [("q", KV_COLS + 512 * i, 512) for i in range(35)] + [("q", KV_COLS + 512 * 35, 256)]
    if self.phases is not None and "p1_small" in self.phases:
        groups = groups[:9]
    import os
    if os.environ.get("P1_GROUPS"):
        groups = [groups[int(i)] for i in os.environ["P1_GROUPS"].split(",")]
    with ExitStack() as ph:
        self.ropeC = kb.sbuf([128, T], F32, "ropeC", ph)
        self.ropeS = kb.sbuf([128, T], F32, "ropeS", ph)
        kb.dma("sp", self.ropeC, self.ropeC[:], ropeC_d, ropeC_d[:, :])
        kb.dma("sp", self.ropeS, self.ropeS[:], ropeS_d, ropeS_d[:, :])
        wring = Ring(kb, [128, 32, 512], BF16, 2, ph, "w")
        aring = Ring(kb, [128, 32, 512], BF16, 2, ph, "a")
        stg = Ring(kb, [128, 4, 512], BF16, 3, ph, "stg")
        rawb = Ring(kb, [128, 512], BF16, 2, ph, "rawb")
        t1r = Ring(kb, [128, 512], F32, 2, ph, "t1")
        t2r = Ring(kb, [128, 512], F32, 2, ph, "t2")
        steps = [(gi, ti) for gi in range(len(groups)) for ti in range(len(TGS))]
        wbuf, abuf = {}, {}

        def prefetch(s):
            gi, ti = steps[s]
            kind, c0, nc_ = groups[gi]
            if ti == 0:
                w = wring.next()
                kb.dma("pool", w, w[:, :, 0:nc_], self.w_in, wv[:, :, c0:c0 + nc_])
                wbuf[gi] = w
            t0, n, v = TGS[ti]
            a = aring.next()
            kb.dma("sp", a, a[:, :, 0:n], self.HT, hv[:, :, t0:t0 + n])
            abuf[s] = a
        prefetch(0)
        for s, (gi, ti) in enumerate(steps):
            if s + 1 < len(steps):
                prefetch(s + 1)
            kind, c0, nc_ = groups[gi]
            t0, n, v = TGS[ti]
            w = wbuf[gi]
            a = abuf.pop(s)
            if kind in ("u", "dv"):
                dst = self.U if kind == "u" else self.DV
                dc0 = c0 if kind == "u" else c0 - 2048
                st = stg.next()
                for tt in range(n // 128):
                    ps = kb.ps()
                    for kc in range(32):
                        kb.op("pe", lambda e: e.matmul(ps[:, 0:nc_], a[:, kc, tt * 128:(tt + 1) * 128], w[:, kc, 0:nc_],
                                                       start=(kc == 0), stop=(kc == 31)), [a, w], [ps])
                    self.copy_evac(st[:, tt, 0:nc_], ps[:, 0:nc_], [ps], [st])
                kb.dma("sp", dst, dst.t[t0:t0 + n, dc0:dc0 + nc_].rearrange("(tt p) c -> p tt c", p=128),
                       st, st[:, 0:n // 128, 0:nc_], nowaw=True)
                continue
            nch = (nc_ + 127) // 128
            M = min(128, nc_)
            pss = [kb.ps() for _ in range(nch)]
            for kc in range(32):
                for j in range(nch):
                    kb.op("pe", lambda e: e.matmul(pss[j][0:M, 0:n], w[:, kc, j * 128:j * 128 + M], a[:, kc, 0:n],
                                                   start=(kc == 0), stop=(kc == 31)), [w, a], [pss[j]])
            st = stg.next()
            qc0 = (c0 - KV_COLS) // 128
            rope = kind in ("dk", "kr") or (kind == "q" and qc0 < QC_CQ)
            for j in range(nch):
                ps = pss[j]
                if not rope:
                    self.copy_evac(st[0:M, j, 0:n], ps[0:M, 0:n], [ps], [st])
                    continue
                rb, t1, t2 = rawb.next(), t1r.next(), t2r.next()
                kb.op("dve", lambda e: e.tensor_copy(rb[0:M, 0:n], ps[0:M, 0:n]), [ps], [rb])
                pp = kb.ps()
                kb.op("pe", lambda e: e.matmul(pp[0:M, 0:n], self.pmat[0:M, 0:M], rb[0:M, 0:n], start=True, stop=True),
                      [self.pmat, rb], [pp])
                kb.op("dve", lambda e: e.tensor_tensor(t1[0:M, 0:n], ps[0:M, 0:n], self.ropeC[0:M, t0:t0 + n], ALU.mult),
                      [ps, self.ropeC], [t1])
                kb.op("dve", lambda e: e.tensor_tensor(t2[0:M, 0:n], pp[0:M, 0:n], self.ropeS[0:M, t0:t0 + n], ALU.mult),
                      [pp, self.ropeS], [t2])
                kb.op("dve", lambda e: e.tensor_tensor(st[0:M, j, 0:n], t1[0:M, 0:n], t2[0:M, 0:n], ALU.add),
                      [t1, t2], [st])
            if kind == "kr":
                kb.dma("sp", self.KR, self.KR.t[:, t0:t0 + n], st, st[0:64, 0, 0:n], nowaw=True)
            else:
                if kind == "dk":
                    dst, ch0 = self.DK, (c0 - 1024) // 128
                elif kind == "ckv":
                    dst, ch0 = self.CKV, 0
                else:
                    dst, ch0 = self.Q, qc0
                kb.dma("sp", dst, dst.t.rearrange("c p t -> p c t")[:, ch0:ch0 + nch, t0:t0 + n],
                       st, st[:, 0:nch, 0:n], nowaw=True)
        kb.end_phase()


Layer.p_proj = _layer_p1


def _layer_p1b(self):
    kb = self.kb
    wuq_d = self.din("w_uq", [768, 3072])
    wukv_d = self.din("w_ukv", [512, 4096])
    gq_d = self.din("gq", [128, 6])
    gkv_d = self.din("gkv", [128, 4])
    self.QN = self.scratch("QN", [16, 128, T])
    self.QR = self.scratch("QR", [16, 64, T])
    self.KN = self.scratch("KN", [16, 128, T])
    self.MV = self.scratch("MV", [T, 2048])
    qv = self.Q.t.rearrange("c p t -> p c t")
    cv = self.CKV.t.rearrange("c p t -> p c t")
    with ExitStack() as ph:
        ropeC = kb.sbuf([64, T], F32, "ropeC", ph)
        ropeS = kb.sbuf([64, T], F32, "ropeS", ph)
        rc_d, rs_d = self.kb_in["ropeC"], self.kb_in["ropeS"]
        kb.dma("sp", ropeC, ropeC[:], rc_d, rc_d[0:64, :])
        kb.dma("sp", ropeS, ropeS[:], rs_d, rs_d[0:64, :])
        wuq = kb.sbuf([128, 6, 3072], BF16, "wuq", ph)
        wukv = kb.sbuf([128, 4, 4096], BF16, "wukv", ph)
        gq = kb.sbuf([128, 6], F32, "gq", ph)
        gkv = kb.sbuf([128, 4], F32, "gkv", ph)
        kb.dma("pool", wuq, wuq[:], wuq_d, wuq_d.t.rearrange("(kc p) n -> p kc n", p=128))
        kb.dma("pool", wukv, wukv[:], wukv_d, wukv_d.t.rearrange("(kc p) n -> p kc n", p=128))
        kb.dma("sp", gq, gq[:], gq_d, gq_d[:, :])
        kb.dma("sp", gkv, gkv[:], gkv_d, gkv_d[:, :])
        cqr = Ring(kb, [128, 6, 512], BF16, 2, ph, "cq")
        ckr = Ring(kb, [128, 4, 512], BF16, 2, ph, "ckv")
        cqn = kb.sbuf([128, 6, 512], BF16, "cqn", ph)
        ckn = kb.sbuf([128, 4, 512], BF16, "ckn", ph)
        sqr = Ring(kb, [128, 512], BF16, 3, ph, "sq")
        rstd = kb.sbuf([128, 512], F32, "rstd", ph)
        stg = Ring(kb, [128, 4, 512], BF16, 3, ph, "stg")
        stq = Ring(kb, [64, 4, 512], BF16, 2, ph, "stq")
        rawb = Ring(kb, [64, 512], BF16, 2, ph, "rawb")
        t1r = Ring(kb, [64, 512], F32, 2, ph, "t1")
        t2r = Ring(kb, [64, 512], F32, 2, ph, "t2")
        wv_v = wukv.t[:, :, :].rearrange("p k (h two d) -> p k h two d", two=2, d=128)
        bufs = {}

        def load(i):
            t0, n, v = TGS[i]
            a, b = cqr.next(), ckr.next()
            kb.dma("sp", a, a[:, :, 0:n], self.Q, qv[:, QC_CQ:QC_CQ + 6, t0:t0 + n])
            kb.dma("sp", b, b[:, :, 0:n], self.CKV, cv[:, :, t0:t0 + n])
            bufs[i] = (a, b)

        def norm(src, dst, g, KC, n):
            ss = kb.ps()
            for kc in range(KC):
                sq = sqr.next()
                kb.op("act", lambda e: e.activation(out=sq[:, 0:n], in_=src[:, kc, 0:n], func=AF.Square), [src], [sq])
                kb.op("pe", lambda e: e.matmul(ss[:, 0:n], self.ones[:], sq[:, 0:n], start=(kc == 0),
                                               stop=(kc == KC - 1)), [self.ones, sq], [ss])
            kb.op("act", lambda e: e.activation(out=rstd[:, 0:n], in_=ss[:, 0:n], func=AF.Sqrt,
                                                scale=1.0 / (KC * 128), bias=self.eps[:]), [ss, self.eps], [rstd])
            kb.op("dve", lambda e: e.reciprocal(rstd[:, 0:n], rstd[:, 0:n]), [], [rstd])
            for kc in range(KC):
                kb.op("dve", lambda e: e.scalar_tensor_tensor(dst[:, kc, 0:n], src[:, kc, 0:n], g[:, kc:kc + 1],
                                                              rstd[:, 0:n], ALU.mult, ALU.mult),
                      [src, g, rstd], [dst])
        load(0)
        for i, (t0, n, v) in enumerate(TGS):
            if i + 1 < len(TGS):
                load(i + 1)
            cq, ck = bufs.pop(i)
            norm(cq, cqn, gq, 6, n)
            norm(ck, ckn, gkv, 4, n)
            for h0 in range(0, 16, 4):
                st, sq4 = stg.next(), stq.next()
                for hh in range(4):
                    h = h0 + hh
                    ps = kb.ps()
                    for kc in range(6):
                        kb.op("pe", lambda e: e.matmul(ps[:, 0:n], wuq[:, kc, h * 192:h * 192 + 128], cqn[:, kc, 0:n],
                                                       start=(kc == 0), stop=(kc == 5)), [wuq, cqn], [ps])
                    self.copy_evac(st[:, hh, 0:n], ps[:, 0:n], [ps], [st])
                    pr = kb.ps()
                    for kc in range(6):
                        kb.op("pe", lambda e: e.matmul(pr[0:64, 0:n], wuq[:, kc, h * 192 + 128:h * 192 + 192],
                                                       cqn[:, kc, 0:n], start=(kc == 0), stop=(kc == 5)),
                              [wuq, cqn], [pr])
                    rb, t1, t2 = rawb.next(), t1r.next(), t2r.next()
                    kb.op("dve", lambda e: e.tensor_copy(rb[:, 0:n], pr[0:64, 0:n]), [pr], [rb])
                    pp = kb.ps()
                    kb.op("pe", lambda e: e.matmul(pp[0:64, 0:n], self.pmat[0:64, 0:64], rb[:, 0:n], start=True,
                                                   stop=True), [self.pmat, rb], [pp])
                    kb.op("dve", lambda e: e.tensor_tensor(t1[:, 0:n], pr[0:64, 0:n], ropeC[:, t0:t0 + n], ALU.mult),
                          [pr, ropeC], [t1])
                    kb.op("dve", lambda e: e.tensor_tensor(t2[:, 0:n], pp[0:64, 0:n], ropeS[:, t0:t0 + n], ALU.mult),
                          [pp, ropeS], [t2])
                    kb.op("dve", lambda e: e.tensor_tensor(sq4[:, hh, 0:n], t1[:, 0:n], t2[:, 0:n], ALU.add),
                          [t1, t2], [sq4])
                kb.dma("sp", self.QN, self.QN.t.rearrange("c p t -> p c t")[:, h0:h0 + 4, t0:t0 + n],
                       st, st[:, :, 0:n], nowaw=True)
                kb.dma("sp", self.QR, self.QR.t.rearrange("c p t -> p c t")[:, h0:h0 + 4, t0:t0 + n],
                       sq4, sq4[:, :, 0:n], nowaw=True)
            for h0 in range(0, 16, 4):
                st = stg.next()
                for hh in range(4):
                    h = h0 + hh
                    ps = kb.ps()
                    for kc in range(4):
                        kb.op("pe", lambda e: e.matmul(ps[:, 0:n], wukv[:, kc, h * 256:h * 256 + 128], ckn[:, kc, 0:n],
                                                       start=(kc == 0), stop=(kc == 3)), [wukv, ckn], [ps])
                    self.copy_evac(st[:, hh, 0:n], ps[:, 0:n], [ps], [st])
                kb.dma("sp", self.KN, self.KN.t.rearrange("c p t -> p c t")[:, h0:h0 + 4, t0:t0 + n],
                       st, st[:, :, 0:n], nowaw=True)
            for h0 in range(0, 16, 4):
                st = stg.next()
                for tt in range(n // 128):
                    ps = kb.ps()
                    for kc in range(4):
                        kb.op("pe", lambda e: e.matmul(ps[:, 0:512], ckn[:, kc, tt * 128:(tt + 1) * 128],
                                                       wv_v[:, kc, h0:h0 + 4, 1, :], start=(kc == 0), stop=(kc == 3)),
                              [ckn, wukv], [ps])
                    self.copy_evac(st[:, tt, :], ps[:, 0:512], [ps], [st])
                kb.dma("sp", self.MV, self.MV.t[t0:t0 + n, h0 * 128:h0 * 128 + 512].rearrange("(tt p) c -> p tt c", p=128),
                       st, st[:, 0:n // 128, :], nowaw=True)
        kb.end_phase()


Layer.p_mla_proj = _layer_p1b


def host_s5(inp, l):
    lre, lim, ldt = inp["s5_lambda_re"][l], inp["s5_lambda_im"][l], inp["s5_log_dt"][l]
    bre, bim, cre, cim, dd = inp["s5_b_re"][l], inp["s5_b_im"][l], inp["s5_c_re"][l], inp["s5_c_im"][l], inp["s5_d"][l]
    lam = np.zeros((128, 3, 64), np.float32)
    BT = np.zeros((64, 2, 128, 128), np.float32)
    CM = np.zeros((64, 2, 128, 128), np.float32)
    for cc in range(8):
        for gp in range(4):
            for d in range(2):
                s = (cc * 4 + gp) * 2 + d
                for gl in range(2):
                    g = 8 * cc + 2 * gp + gl
                    lam[gl * 64:(gl + 1) * 64, 0, s] = lre[d, g]
                    lam[gl * 64:(gl + 1) * 64, 1, s] = lim[d, g]
                    lam[gl * 64:(gl + 1) * 64, 2, s] = ldt[d, g]
                    c0 = (2 * gp + gl) * 16
                    BT[s, 0, c0:c0 + 16, gl * 64:(gl + 1) * 64] = bre[d, g].T
                    BT[s, 1, c0:c0 + 16, gl * 64:(gl + 1) * 64] = bim[d, g].T
                    CM[s, 0, gl * 64:(gl + 1) * 64, c0:c0 + 16] = cre[d, g].T
                    CM[s, 1, gl * 64:(gl + 1) * 64, c0:c0 + 16] = cim[d, g].T
    DG = np.zeros((8, 128, 128), np.float32)
    dflat = dd.reshape(8, 128)
    for cc in range(8):
        DG[cc][np.arange(128), np.arange(128)] = dflat[cc]
    return {"s5_lam": lam, "s5_BT": BT, "s5_CM": CM, "s5_DG": DG}


def _layer_s5(self):
    kb = self.kb
    lam_d = self.din("s5_lam", [128, 3, 64])
    BT_d = self.din("s5_BT", [64, 2, 128, 128])
    CM_d = self.din("s5_CM", [64, 2, 128, 128])
    DG_d = self.din("s5_DG", [8, 128, 128])
    self.YA = self.scratch("YA", [T, 1024])
    TC = 512
    chunks = [(i * TC, min(TC, T - i * TC)) for i in range((T + TC - 1) // TC)]
    TWO_PI = 2.0 * math.pi
    ccs = range(8) if not (self.phases and "s5_small" in self.phases) else range(1)
    with ExitStack() as ph:
        sb = lambda shape, dt=F32, name="s5": kb.sbuf(shape, dt, name, ph)
        lam = sb([128, 3, 64])
        kb.dma("sp", lam, lam[:], lam_d, lam_d[:, :, :])
        V = {k: sb([128, 64]) for k in ("dt", "lr", "er", "th", "k", "s4", "c4", "s2", "c2", "st", "ct", "are", "aim",
                                        "nr", "den", "fr", "fi", "tmp", "tmp2")}
        ki = sb([128, 64], mybir.dt.int32)
        dve = lambda fn, r, w: kb.op("dve", fn, r, w)
        act = lambda fn, r, w: kb.op("act", fn, r, w)
        A = lambda k: V[k][:]
        act(lambda e: e.activation(out=A("dt"), in_=lam[:, 2, :], func=AF.Exp), [lam], [V["dt"]])
        dve(lambda e: e.tensor_scalar(A("lr"), lam[:, 0, :], -1e-4, None, ALU.min), [lam], [V["lr"]])
        dve(lambda e: e.tensor_tensor(A("tmp"), A("lr"), A("dt"), ALU.mult), [V["lr"], V["dt"]], [V["tmp"]])
        act(lambda e: e.activation(out=A("er"), in_=A("tmp"), func=AF.Exp), [V["tmp"]], [V["er"]])
        dve(lambda e: e.tensor_tensor(A("th"), lam[:, 1, :], A("dt"), ALU.mult), [lam, V["dt"]], [V["th"]])
        dve(lambda e: e.tensor_scalar(A("tmp"), A("th"), 1.0 / TWO_PI, None, ALU.mult), [V["th"]], [V["tmp"]])
        dve(lambda e: e.tensor_copy(ki[:], A("tmp")), [V["tmp"]], [ki])
        dve(lambda e: e.tensor_copy(A("k"), ki[:]), [ki], [V["k"]])
        dve(lambda e: e.scalar_tensor_tensor(A("tmp"), A("k"), -TWO_PI, A("th"), ALU.mult, ALU.add),
            [V["k"], V["th"]], [V["tmp"]])
        act(lambda e: e.activation(out=A("s4"), in_=A("tmp"), func=AF.Sin, scale=0.25), [V["tmp"]], [V["s4"]])
        dve(lambda e: e.tensor_tensor(A("tmp2"), A("s4"), A("s4"), ALU.mult), [V["s4"]], [V["tmp2"]])
        dve(lambda e: e.tensor_scalar(A("c2"), A("tmp2"), -2.0, 1.0, ALU.mult, ALU.add), [V["tmp2"]], [V["c2"]])
        dve(lambda e: e.tensor_scalar(A("tmp2"), A("tmp2"), -1.0, 1.0, ALU.mult, ALU.add), [], [V["tmp2"]])
        act(lambda e: e.activation(out=A("c4"), in_=A("tmp2"), func=AF.Sqrt), [V["tmp2"]], [V["c4"]])
        dve(lambda e: e.scalar_tensor_tensor(A("s2"), A("s4"), 2.0, A("c4"), ALU.mult, ALU.mult),
            [V["s4"], V["c4"]], [V["s2"]])
        dve(lambda e: e.scalar_tensor_tensor(A("st"), A("s2"), 2.0, A("c2"), ALU.mult, ALU.mult),
            [V["s2"], V["c2"]], [V["st"]])
        dve(lambda e: e.tensor_tensor(A("tmp"), A("s2"), A("s2"), ALU.mult), [V["s2"]], [V["tmp"]])
        dve(lambda e: e.tensor_scalar(A("ct"), A("tmp"), -2.0, 1.0, ALU.mult, ALU.add), [V["tmp"]], [V["ct"]])
        dve(lambda e: e.tensor_tensor(A("are"), A("er"), A("ct"), ALU.mult), [V["er"], V["ct"]], [V["are"]])
        dve(lambda e: e.tensor_tensor(A("aim"), A("er"), A("st"), ALU.mult), [V["er"], V["st"]], [V["aim"]])
        dve(lambda e: e.tensor_scalar(A("nr"), A("are"), -1.0, None, ALU.add), [V["are"]], [V["nr"]])
        dve(lambda e: e.tensor_tensor(A("den"), A("lr"), A("lr"), ALU.mult), [V["lr"]], [V["den"]])
        dve(lambda e: e.tensor_tensor(A("tmp"), lam[:, 1, :], lam[:, 1, :], ALU.mult), [lam], [V["tmp"]])
        dve(lambda e: e.tensor_tensor(A("den"), A("den"), A("tmp"), ALU.add), [V["tmp"]], [V["den"]])
        dve(lambda e: e.reciprocal(A("den"), A("den")), [], [V["den"]])
        dve(lambda e: e.tensor_tensor(A("tmp"), A("nr"), A("lr"), ALU.mult), [V["nr"], V["lr"]], [V["tmp"]])
        dve(lambda e: e.tensor_tensor(A("tmp2"), A("aim"), lam[:, 1, :], ALU.mult), [V["aim"], lam], [V["tmp2"]])
        dve(lambda e: e.tensor_tensor(A("tmp"), A("tmp"), A("tmp2"), ALU.add), [V["tmp2"]], [V["tmp"]])
        dve(lambda e: e.tensor_tensor(A("fr"), A("tmp"), A("den"), ALU.mult), [V["tmp"], V["den"]], [V["fr"]])
        dve(lambda e: e.tensor_tensor(A("tmp"), A("aim"), A("lr"), ALU.mult), [V["aim"], V["lr"]], [V["tmp"]])
        dve(lambda e: e.tensor_tensor(A("tmp2"), A("nr"), lam[:, 1, :], ALU.mult), [V["nr"], lam], [V["tmp2"]])
        dve(lambda e: e.tensor_tensor(A("tmp"), A("tmp"), A("tmp2"), ALU.subtract), [V["tmp2"]], [V["tmp"]])
        dve(lambda e: e.tensor_tensor(A("fi"), A("tmp"), A("den"), ALU.mult), [V["tmp"], V["den"]], [V["fi"]])

        onesf = sb([128, TC])
        dve(lambda e: e.memset(onesf[:], 1.0), [], [onesf])
        Ec = [sb([128, TC], name="Ec") for _ in range(8)]
        Es = [sb([128, TC], name="Es") for _ in range(8)]
        Dc = [sb([128, TC], name="Dc") for _ in range(8)]
        Ds = [sb([128, TC], name="Ds") for _ in range(8)]
        Rt = [sb([128, TC], name="Rt") for _ in range(8)]
        nEs = [sb([128, TC], name="nEs") for _ in range(8)]
        ETC = [sb([128, 2], name="ETC") for _ in range(8)]
        em = sb([128, 4])
        ttab = sb([128, TC])
        BTs = [sb([128, 2, 128], BF16, "BT") for _ in range(8)]
        CMs = [sb([128, 2, 128], BF16, "CM") for _ in range(8)]
        DG = sb([128, 128], BF16, "DG")
        useq = [sb([128, T], BF16, "useq") for _ in range(2)]
        yR = sb([128, 34, 128], BF16, "yR")
        utile = Ring(kb, [128, 128], BF16, 3, ph, "utile")
        wr = Ring(kb, [128, TC], F32, 2, ph, "wr")
        wi = Ring(kb, [128, TC], F32, 2, ph, "wi")
        zr = Ring(kb, [128, TC], F32, 2, ph, "zr")
        zi = Ring(kb, [128, TC], F32, 2, ph, "zi")
        ta = Ring(kb, [128, TC], F32, 2, ph, "ta")
        tb = Ring(kb, [128, TC], F32, 2, ph, "tb")
        pa = Ring(kb, [128, TC], F32, 2, ph, "pa")
        pb = Ring(kb, [128, TC], F32, 2, ph, "pb")
        xr = Ring(kb, [128, TC], BF16, 8, ph, "xr")
        xi = Ring(kb, [128, TC], BF16, 8, ph, "xi")
        car = [sb([128, 2], name="car") for _ in range(4)]
        yst = Ring(kb, [128, 4, 128], BF16, 2, ph, "yst")
        for cc in ccs:
            kb.dma("pool", DG, DG[:], DG_d, DG_d[cc])
            for q in range(8):
                s = cc * 8 + q
                kb.dma("pool", BTs[q], BTs[q][:], BT_d, BT_d.t[s].rearrange("r c p -> c r p"))
                kb.dma("pool", CMs[q], CMs[q][:], CM_d, CM_d.t[s].rearrange("r p c -> p r c"))
                col = lambda k: V[k][:, s:s + 1]
                dve(lambda e: e.memset(Ec[q][:, 0:1], 1.0), [], [Ec[q]])
                dve(lambda e: e.memset(Es[q][:, 0:1], 0.0), [], [Es[q]])
                dve(lambda e: e.tensor_copy(em[:, 0:1], col("ct")), [V["ct"]], [em])
                dve(lambda e: e.tensor_copy(em[:, 1:2], col("st")), [V["st"]], [em])
                m = 1
                while m <= TC:
                    if m > 1:
                        h = m // 2
                        dve(lambda e: e.tensor_tensor(em[:, 2:3], Es[q][:, h:h + 1], Es[q][:, h:h + 1], ALU.mult), [Es[q]], [em])
                        dve(lambda e: e.scalar_tensor_tensor(em[:, 0:1], Ec[q][:, h:h + 1], Ec[q][:, h:h + 1], em[:, 2:3],
                                                             ALU.mult, ALU.subtract), [Ec[q]], [em])
                        dve(lambda e: e.scalar_tensor_tensor(em[:, 1:2], Ec[q][:, h:h + 1], 2.0, Es[q][:, h:h + 1],
                                                             ALU.mult, ALU.mult), [Ec[q], Es[q]], [em])
                    if m == TC:
                        dve(lambda e: e.tensor_copy(ETC[q][:], em[:, 0:2]), [em], [ETC[q]])
                        break
                    dve(lambda e: e.tensor_scalar(ttab[:, 0:m], Es[q][:, 0:m], em[:, 1:2], None, ALU.mult), [Es[q], em], [ttab])
                    dve(lambda e: e.scalar_tensor_tensor(Ec[q][:, m:2 * m], Ec[q][:, 0:m], em[:, 0:1], ttab[:, 0:m],
                                                         ALU.mult, ALU.subtract), [ttab, em], [Ec[q]])
                    dve(lambda e: e.tensor_scalar(ttab[:, 0:m], Es[q][:, 0:m], em[:, 0:1], None, ALU.mult), [Es[q], em], [ttab])
                    dve(lambda e: e.scalar_tensor_tensor(Es[q][:, m:2 * m], Ec[q][:, 0:m], em[:, 1:2], ttab[:, 0:m],
                                                         ALU.mult, ALU.add), [ttab, em, Ec[q]], [Es[q]])
                    m *= 2
                dve(lambda e: e.tensor_scalar(ttab[:], Es[q][:], col("fi"), None, ALU.mult), [Es[q], V["fi"]], [ttab])
                dve(lambda e: e.scalar_tensor_tensor(Dc[q][:], Ec[q][:], col("fr"), ttab[:], ALU.mult, ALU.add),
                    [Ec[q], V["fr"], ttab], [Dc[q]])
                dve(lambda e: e.tensor_scalar(ttab[:], Es[q][:], col("fr"), None, ALU.mult), [Es[q], V["fr"]], [ttab])
                dve(lambda e: e.scalar_tensor_tensor(Ds[q][:], Ec[q][:], col("fi"), ttab[:], ALU.mult, ALU.subtract),
                    [Ec[q], V["fi"], ttab], [Ds[q]])
                dve(lambda e: e.tensor_scalar(Rt[q][:], onesf[:], col("er"), None, ALU.mult), [onesf, V["er"]], [Rt[q]])
                dve(lambda e: e.tensor_scalar(nEs[q][:], Es[q][:], -1.0, None, ALU.mult), [Es[q]], [nEs[q]])
            if "S5DBG" in self.dbg and cc == 0:
                dbg1 = kb.dram("S5DBG", [128, 19, 64], F32, "ExternalOutput")
                for ki_, k_ in enumerate(sorted(V)):
                    kb.dma("sp", dbg1, dbg1.t[:, ki_, :], V[k_], V[k_][:], nowaw=True)
                dbg2 = kb.dram("S5TAB", [128, 4, TC], F32, "ExternalOutput")
                for ki_, tb_ in enumerate((Ec[0], Es[0], Dc[0], Ds[0])):
                    kb.dma("sp", dbg2, dbg2.t[:, ki_, :], tb_, tb_[:], nowaw=True)
            for d in range(2):
                for r in range(34):
                    src = r if d == 0 else (1 - r if r < 2 else 35 - r)
                    ut = utile.next()
                    kb.dma("sp", ut, ut[:], self.U, self.U.t[src * 128:(src + 1) * 128, cc * 128:(cc + 1) * 128])
                    ps = kb.ps()
                    kb.op("pe", lambda e: e.matmul(ps[:, 0:128], ut[:], (self.ident if d == 0 else self.jmat)[:],
                                                   start=True, stop=True), [ut, self.ident, self.jmat], [ps])
                    self.copy_evac(useq[d][:, r * 128:(r + 1) * 128], ps[:, 0:128], [ps], [useq[d]])
            for d in (1, 0):
                for ci, (c0, n) in enumerate(chunks):
                    ntt = n // 128
                    xs = []
                    for gp in range(4):
                        q = gp * 2 + d
                        pbr, pbi = kb.ps(), kb.ps()
                        kb.op("pe", lambda e: e.matmul(pbr[:, 0:n], BTs[q][:, 0, :], useq[d][:, c0:c0 + n], start=True, stop=True),
                              [BTs[q], useq[d]], [pbr])
                        kb.op("pe", lambda e: e.matmul(pbi[:, 0:n], BTs[q][:, 1, :], useq[d][:, c0:c0 + n], start=True, stop=True),
                              [BTs[q], useq[d]], [pbi])
                        t1, t2, w_r, w_i = ta.next(), tb.next(), wr.next(), wi.next()
                        dve(lambda e: e.tensor_tensor(t1[:, 0:n], pbr[:, 0:n], Dc[q][:, 0:n], ALU.mult), [pbr, Dc[q]], [t1])
                        dve(lambda e: e.tensor_tensor(t2[:, 0:n], pbi[:, 0:n], Ds[q][:, 0:n], ALU.mult), [pbi, Ds[q]], [t2])
                        dve(lambda e: e.tensor_tensor(w_r[:, 0:n], t1[:, 0:n], t2[:, 0:n], ALU.subtract), [t1, t2], [w_r])
                        dve(lambda e: e.tensor_tensor(t1[:, 0:n], pbr[:, 0:n], Ds[q][:, 0:n], ALU.mult), [pbr, Ds[q]], [t1])
                        dve(lambda e: e.tensor_tensor(t2[:, 0:n], pbi[:, 0:n], Dc[q][:, 0:n], ALU.mult), [pbi, Dc[q]], [t2])
                        dve(lambda e: e.tensor_tensor(w_i[:, 0:n], t1[:, 0:n], t2[:, 0:n], ALU.add), [t1, t2], [w_i])
                        z_r, z_i = zr.next(), zi.next()
                        if ci == 0:
                            i0r, i0i, ib = 0.0, 0.0, []
                        else:
                            i0r, i0i, ib = car[gp][:, 0:1], car[gp][:, 1:2], [car[gp]]
                        dve(lambda e: e.tensor_tensor_scan(z_r[:, 0:n], Rt[q][:, 0:n], w_r[:, 0:n], i0r, ALU.mult, ALU.add),
                            [Rt[q], w_r] + ib, [z_r])
                        dve(lambda e: e.tensor_tensor_scan(z_i[:, 0:n], Rt[q][:, 0:n], w_i[:, 0:n], i0i, ALU.mult, ALU.add),
                            [Rt[q], w_i] + ib, [z_i])
                        if n == TC:
                            dve(lambda e: e.tensor_tensor(em[:, 3:4], z_i[:, TC - 1:TC], ETC[q][:, 1:2], ALU.mult), [z_i, ETC[q]], [em])
                            dve(lambda e: e.scalar_tensor_tensor(car[gp][:, 0:1], z_r[:, TC - 1:TC], ETC[q][:, 0:1], em[:, 3:4],
                                                                 ALU.mult, ALU.subtract), [z_r, ETC[q], em], [car[gp]])
                            dve(lambda e: e.tensor_tensor(em[:, 3:4], z_i[:, TC - 1:TC], ETC[q][:, 0:1], ALU.mult), [z_i, ETC[q]], [em])
                            dve(lambda e: e.scalar_tensor_tensor(car[gp][:, 1:2], z_r[:, TC - 1:TC], ETC[q][:, 1:2], em[:, 3:4],
                                                                 ALU.mult, ALU.add), [z_r, ETC[q], em], [car[gp]])
                        p1, p2, x_r, x_i = pa.next(), pb.next(), xr.next(), xi.next()
                        pool = lambda fn, r, w: kb.op("pool", fn, r, w)
                        pool(lambda e: e.tensor_tensor(p1[:, 0:n], z_r[:, 0:n], Ec[q][:, 0:n], ALU.mult), [z_r, Ec[q]], [p1])
                        pool(lambda e: e.tensor_tensor(p2[:, 0:n], z_i[:, 0:n], Es[q][:, 0:n], ALU.mult), [z_i, Es[q]], [p2])
                        pool(lambda e: e.tensor_tensor(x_r[:, 0:n], p1[:, 0:n], p2[:, 0:n], ALU.subtract), [p1, p2], [x_r])
                        pool(lambda e: e.tensor_tensor(p1[:, 0:n], z_r[:, 0:n], nEs[q][:, 0:n], ALU.mult), [z_r, nEs[q]], [p1])
                        pool(lambda e: e.tensor_tensor(p2[:, 0:n], z_i[:, 0:n], Ec[q][:, 0:n], ALU.mult), [z_i, Ec[q]], [p2])
                        pool(lambda e: e.tensor_tensor(x_i[:, 0:n], p1[:, 0:n], p2[:, 0:n], ALU.subtract), [p1, p2], [x_i])
                        if "S5X" in self.dbg and cc == 0 and d == 0 and ci == 0 and gp == 0:
                            dbx = kb.dram("S5X", [128, 6, TC], F32, "ExternalOutput")
                            xf = sb([128, 2, TC], name="xf")
                            dve(lambda e: e.tensor_copy(xf[:, 0, 0:n], x_r[:, 0:n]), [x_r], [xf])
                            dve(lambda e: e.tensor_copy(xf[:, 1, 0:n], x_i[:, 0:n]), [x_i], [xf])
                            for ki_, tb_ in enumerate((w_r, w_i, z_r, z_i)):
                                kb.dma("sp", dbx, dbx.t[:, ki_, 0:n], tb_, tb_[:, 0:n], nowaw=True)
                            kb.dma("sp", dbx, dbx.t[:, 4:6, 0:n], xf, xf[:, :, 0:n], nowaw=True)
                        xs.append((x_r, x_i, q))
                    py = kb.ps()
                    ys = yst.next()
                    for tt in range(ntt):
                        sl = slice(tt * 128, (tt + 1) * 128)
                        for gi_, (x_r, x_i, q) in enumerate(xs):
                            kb.op("pe", lambda e: e.matmul(py[:, sl], x_r[:, sl], CMs[q][:, 0, :], start=(gi_ == 0), stop=False),
                                  [x_r, CMs[q]], [py])
                            kb.op("pe", lambda e: e.matmul(py[:, sl], x_i[:, sl], CMs[q][:, 1, :], start=False,
                                                           stop=(gi_ == 3 and d == 1)), [x_i, CMs[q]], [py])
                        if d == 1:
                            r = c0 // 128 + tt
                            self.copy_evac(yR[:, r, :], py[:, sl], [py], [yR])
                        else:
                            ti = c0 // 128 + tt
                            r = (1 - ti) if ti < 2 else (35 - ti)
                            kb.op("pe", lambda e: e.matmul(py[:, sl], self.jmat[:], yR[:, r, :], start=False, stop=False),
                                  [self.jmat, yR], [py])
                            kb.op("pe", lambda e: e.matmul(py[:, sl], useq[0][:, ti * 128:(ti + 1) * 128], DG[:], start=False,
                                                           stop=True), [useq[0], DG], [py])
                            self.copy_evac(ys[:, tt, :], py[:, sl], [py], [ys])
                    if d == 0:
                        kb.dma("sp", self.YA, self.YA.t[c0:c0 + n, cc * 128:(cc + 1) * 128].rearrange("(tt p) c -> p tt c", p=128),
                               ys, ys[:, 0:ntt, :], nowaw=True)
        kb.end_phase()


Layer.p_s5 = _layer_s5


def _layer_attn(self):
    kb = self.kb
    dl_d = self.din("dlam", [128, 4, 64])
    gd_d = self.din("gdiff", [128, 1])
    li_d = self.din("lam_init", [128, 2])
    self.ZB = self.scratch("ZB", [8, 128, T])
    self.ZC = self.scratch("ZC", [16, 128, T])
    small = self.phases is not None and "attn_small" in self.phases
    qv = self.Q.t.rearrange("c p t -> p c t")
    with ExitStack() as ph:
        sb = lambda shape, dt=F32, name="at": kb.sbuf(shape, dt, name, ph)
        dve = lambda fn, r, w: kb.op("dve", fn, r, w)
        act = lambda fn, r, w: kb.op("act", fn, r, w)
        dl = sb([128, 4, 64]); gdw = sb([128, 1]); li = sb([128, 2]); lt = sb([128, 64]); lv = sb([128, 4])
        kb.dma("sp", dl, dl[:], dl_d, dl_d[:, :, :])
        kb.dma("sp", gdw, gdw[:], gd_d, gd_d[:, :])
        kb.dma("sp", li, li[:], li_d, li_d[:, :])
        for i in range(2):
            dve(lambda e: e.tensor_tensor(lt[:], dl[:, 2 * i, :], dl[:, 2 * i + 1, :], ALU.mult), [dl], [lt])
            dve(lambda e: e.reduce_sum(lv[:, i:i + 1], lt[:], axis=AX.X), [lt], [lv])
        act(lambda e: e.activation(out=lv[:, 0:2], in_=lv[:, 0:2], func=AF.Exp), [], [lv])
        dve(lambda e: e.tensor_tensor(lv[:, 2:3], lv[:, 1:2], lv[:, 0:1], ALU.subtract), [], [lv])
        dve(lambda e: e.tensor_tensor(lv[:, 3:4], lv[:, 2:3], li[:, 0:1], ALU.subtract), [li], [lv])
        dve(lambda e: e.tensor_scalar(gdw[:], gdw[:], li[:, 1:2], None, ALU.mult), [li], [gdw])
        neg_lam = lv[:, 3:4]
        PT = [sb([128, 34, 512], BF16, "PT") for _ in range(2)]
        kring = Ring(kb, [128, T], BF16, 2, ph, "kT")
        vring = Ring(kb, [128, 34, 129], BF16, 2, ph, "v")
        kr = sb([64, T], BF16, "kr")
        kb.dma("sp", kr, kr[:], self.KR, self.KR.t[:, :])
        qring = Ring(kb, [128, 512], BF16, 3, ph, "q")
        qrr = Ring(kb, [64, 512], BF16, 3, ph, "qr")
        gring = Ring(kb, [128, 512], BF16, 3, ph, "g")
        sgr = Ring(kb, [128, 512], F32, 2, ph, "sg")
        rdr = Ring(kb, [128, 512], F32, 3, ph, "rd")
        ofr = Ring(kb, [128, 512], F32, 5, ph, "of")
        sqb = Ring(kb, [128, 512], BF16, 2, ph, "sqb")
        zst = Ring(kb, [128, 512], BF16, 2, ph, "zst")
        for v in vring.bufs:
            dve(lambda e: e.memset(v[:, :, 128:129], 1.0), [], [v])

        def head(kind, h):
            nmap = 2 if kind == "diff" else 1
            scale = 64 ** -0.5 if kind == "diff" else 192 ** -0.5
            kT, vt = kring.next(), vring.next()
            if kind == "diff":
                kb.dma("sp", kT, kT[:], self.DK, self.DK.t[h])
                kb.dma("sp", vt, vt[:, :, 0:128], self.DV,
                       self.DV.t[:, h * 128:(h + 1) * 128].rearrange("(kc p) c -> p kc c", p=128))
                gch, zdst = QC_GB + h, self.ZB
            else:
                kb.dma("sp", kT, kT[:], self.KN, self.KN.t[h])
                kb.dma("sp", vt, vt[:, :, 0:128], self.MV,
                       self.MV.t[:, h * 128:(h + 1) * 128].rearrange("(kc p) c -> p kc c", p=128))
                gch, zdst = QC_GC + h, self.ZC
            for ti, (t0, n, _) in enumerate(TGS):
                if small and ti > 1:
                    break
                nk = 2 if ti == 0 else 34
                qb, gb = qring.next(), gring.next()
                if kind == "diff":
                    kb.dma("sp", qb, qb[:, 0:n], self.Q, self.Q.t[QC_DQ + h, :, t0:t0 + n])
                else:
                    qr_ = qrr.next()
                    kb.dma("sp", qb, qb[:, 0:n], self.QN, self.QN.t[h, :, t0:t0 + n])
                    kb.dma("sp", qr_, qr_[:, 0:n], self.QR, self.QR.t[h, :, t0:t0 + n])
                kb.dma("sp", gb, gb[:, 0:n], self.Q, self.Q.t[gch, :, t0:t0 + n])
                sg = sgr.next()
                act(lambda e: e.activation(out=sg[:, 0:n], in_=gb[:, 0:n], func=AF.Silu), [gb], [sg])
                for m in range(nmap):
                    pt = PT[m] if kind == "diff" else PT[ti % 2]
                    for kc in range(nk):
                        ps = kb.ps()
                        ksl = slice(kc * 128, (kc + 1) * 128)
                        if kind == "diff":
                            kb.op("pe", lambda e: e.matmul(ps[:, 0:n], kT[m * 64:(m + 1) * 64, ksl], qb[m * 64:(m + 1) * 64, 0:n],
                                                           start=True, stop=True), [kT, qb], [ps])
                        else:
                            kb.op("pe", lambda e: e.matmul(ps[:, 0:n], kT[:, ksl], qb[:, 0:n], start=True, stop=False),
                                  [kT, qb], [ps])
                            kb.op("pe", lambda e: e.matmul(ps[:, 0:n], kr[:, ksl], qr_[:, 0:n], start=False, stop=True),
                                  [kr, qr_], [ps])
                        act(lambda e: e.activation(out=pt[:, kc, 0:n], in_=ps[:, 0:n], func=AF.Exp, scale=scale), [ps], [pt])
                zs = zst.next()
                om = []
                for m in range(nmap):
                    pt = PT[m] if kind == "diff" else PT[ti % 2]
                    po, pd = kb.ps(), kb.ps()
                    for kc in range(nk):
                        kb.op("pe", lambda e: e.matmul(po[:, 0:n], vt[:, kc, 0:128], pt[:, kc, 0:n], start=(kc == 0),
                                                       stop=(kc == nk - 1)), [vt, pt], [po])
                        kb.op("pe", lambda e: e.matmul(pd[:, 0:n], self.ones[:], pt[:, kc, 0:n], start=(kc == 0),
                                                       stop=(kc == nk - 1)), [self.ones, pt], [pd])
                    rd, o = rdr.next(), ofr.next()
                    dve(lambda e: e.reciprocal(rd[:, 0:n], pd[:, 0:n]), [pd], [rd])
                    dve(lambda e: e.tensor_tensor(o[:, 0:n], po[:, 0:n], rd[:, 0:n], ALU.mult), [po, rd], [o])
                    om.append(o)
                if kind == "diff":
                    ob, sq = ofr.next(), sqb.next()
                    dve(lambda e: e.scalar_tensor_tensor(ob[:, 0:n], om[1][:, 0:n], neg_lam, om[0][:, 0:n], ALU.mult, ALU.add),
                        [om[0], om[1], lv], [ob])
                    act(lambda e: e.activation(out=sq[:, 0:n], in_=ob[:, 0:n], func=AF.Square), [ob], [sq])
                    pss_ = kb.ps()
                    kb.op("pe", lambda e: e.matmul(pss_[:, 0:n], self.ones[:], sq[:, 0:n], start=True, stop=True),
                          [self.ones, sq], [pss_])
                    rs = rdr.next()
                    act(lambda e: e.activation(out=rs[:, 0:n], in_=pss_[:, 0:n], func=AF.Sqrt, scale=1.0 / 128, bias=self.eps[:]),
                        [pss_, self.eps], [rs])
                    dve(lambda e: e.reciprocal(rs[:, 0:n], rs[:, 0:n]), [], [rs])
                    t_ = ofr.next()
                    dve(lambda e: e.scalar_tensor_tensor(t_[:, 0:n], ob[:, 0:n], gdw[:, 0:1], rs[:, 0:n], ALU.mult, ALU.mult),
                        [ob, gdw, rs], [t_])
                    dve(lambda e: e.tensor_tensor(zs[:, 0:n], t_[:, 0:n], sg[:, 0:n], ALU.mult), [t_, sg], [zs])
                else:
                    dve(lambda e: e.tensor_tensor(zs[:, 0:n], om[0][:, 0:n], sg[:, 0:n], ALU.mult), [om[0], sg], [zs])
                kb.dma("sp", zdst, zdst.t[h, :, t0:t0 + n], zs, zs[:, 0:n], nowaw=True)

        heads = [("diff", h) for h in range(8)] + [("mla", h) for h in range(16)]
        if small:
            heads = [("diff", 0), ("mla", 0)]
        for kind, h in heads:
            head(kind, h)
        kb.end_phase()


Layer.p_attn = _layer_attn


def _layer_s5fin(self):
    kb = self.kb
    wglu_d = self.din("w_glu", [1024, 1024])
    self.ZA = self.scratch("ZA", [8, 128, T])
    G1 = 2.0 * math.sqrt(2.0 / math.pi)
    with ExitStack() as ph:
        sb = lambda shape, dt=F32, name="sf": kb.sbuf(shape, dt, name, ph)
        dve = lambda fn, r, w: kb.op("dve", fn, r, w)
        act = lambda fn, r, w: kb.op("act", fn, r, w)
        wglu = sb([128, 8, 1024], BF16, "wglu")
        kb.dma("pool", wglu, wglu[:], wglu_d, wglu_d.t.rearrange("(kc p) n -> p kc n", p=128))
        yar = Ring(kb, [128, 4, 1024], BF16, 2, ph, "ya")
        gar = Ring(kb, [128, 8, 512], BF16, 2, ph, "ga")
        zg = sb([128, 8, 512], BF16, "zg")
        xs, x2, vv, sg = sb([128, 512]), sb([128, 512]), sb([128, 512]), sb([128, 512])
        sgl, sga, tt_ = sb([128, 512]), sb([128, 512]), sb([128, 512])
        zst = Ring(kb, [128, 8, 512], BF16, 2, ph, "zst")
        qv = self.Q.t.rearrange("c p t -> p c t")
        bufs = {}

        def load(i):
            t0, n, _ = TGS[i]
            a, g = yar.next(), gar.next()
            kb.dma("sp", a, a[:, 0:n // 128, :], self.YA, self.YA.t[t0:t0 + n, :].rearrange("(tt p) c -> p tt c", p=128))
            kb.dma("sp", g, g[:, :, 0:n], self.Q, qv[:, QC_GA:QC_GA + 8, t0:t0 + n])
            bufs[i] = (a, g)
        load(0)
        for i, (t0, n, _) in enumerate(TGS):
            if i + 1 < len(TGS):
                load(i + 1)
            ya, ga = bufs.pop(i)
            for c in range(8):
                ps = kb.ps()
                for tt in range(n // 128):
                    kb.op("pe", lambda e: e.matmul(ps[:, tt * 128:(tt + 1) * 128], ya[:, tt, c * 128:(c + 1) * 128], self.ident[:],
                                                   start=True, stop=True), [ya, self.ident], [ps])
                dve(lambda e: e.tensor_copy(xs[:, 0:n], ps[:, 0:n]), [ps], [xs])
                act(lambda e: e.activation(out=x2[:, 0:n], in_=xs[:, 0:n], func=AF.Square), [xs], [x2])
                dve(lambda e: e.tensor_scalar(x2[:, 0:n], x2[:, 0:n], 0.044715, 1.0, ALU.mult, ALU.add), [], [x2])
                dve(lambda e: e.tensor_tensor(vv[:, 0:n], x2[:, 0:n], xs[:, 0:n], ALU.mult), [x2, xs], [vv])
                act(lambda e: e.activation(out=sg[:, 0:n], in_=vv[:, 0:n], func=AF.Sigmoid, scale=G1), [vv], [sg])
                dve(lambda e: e.tensor_tensor(zg[:, c, 0:n], xs[:, 0:n], sg[:, 0:n], ALU.mult), [xs, sg], [zg])
            zs = zst.next()
            for c2 in range(8):
                ps = kb.ps()
                for c in range(8):
                    kb.op("pe", lambda e: e.matmul(ps[:, 0:n], wglu[:, c, c2 * 128:(c2 + 1) * 128], zg[:, c, 0:n],
                                                   start=(c == 0), stop=(c == 7)), [wglu, zg], [ps])
                act(lambda e: e.activation(out=sgl[:, 0:n], in_=ps[:, 0:n], func=AF.Sigmoid), [ps], [sgl])
                act(lambda e: e.activation(out=sga[:, 0:n], in_=ga[:, c2, 0:n], func=AF.Silu), [ga], [sga])
                dve(lambda e: e.tensor_tensor(tt_[:, 0:n], zg[:, c2, 0:n], sgl[:, 0:n], ALU.mult), [zg, sgl], [tt_])
                dve(lambda e: e.tensor_tensor(zs[:, c2, 0:n], tt_[:, 0:n], sga[:, 0:n], ALU.mult), [tt_, sga], [zs])
            kb.dma("sp", self.ZA, self.ZA.t.rearrange("c p t -> p c t")[:, :, t0:t0 + n], zs, zs[:, :, 0:n], nowaw=True)
        kb.end_phase()


def _layer_merge(self):
    kb = self.kb
    wa_d = self.din("w_br_s5", [1024, D])
    wb_d = self.din("w_br_diff", [1024, D])
    wc_d = self.din("w_br_mla", [2048, D])
    self.Y = self.scratch("Y", [32, 128, T])
    with ExitStack() as ph:
        sb = lambda shape, dt=F32, name="mg": kb.sbuf(shape, dt, name, ph)
        dve = lambda fn, r, w: kb.op("dve", fn, r, w)
        act = lambda fn, r, w: kb.op("act", fn, r, w)
        wring = Ring(kb, [128, 32, 512], BF16, 2, ph, "w")
        aring = Ring(kb, [128, 32, 512], BF16, 2, ph, "a")
        gring = Ring(kb, [128, 3, 4, 512], BF16, 2, ph, "gm")
        sgr = Ring(kb, [128, 3, 512], F32, 2, ph, "sig")
        t1, t2 = sb([128, 512]), sb([128, 512])
        stg = Ring(kb, [128, 4, 512], BF16, 2, ph, "stg")
        qv = self.Q.t.rearrange("c p t -> p c t")
        zav, zbv, zcv = (z.t.rearrange("c p t -> p c t") for z in (self.ZA, self.ZB, self.ZC))
        steps = [(cg, ti) for cg in range(8) for ti in range(len(TGS))]
        wb_, ab_ = {}, {}

        def prefetch(s):
            cg, ti = steps[s]
            c0 = cg * 512
            if ti == 0:
                w = wring.next()
                kb.dma("pool", w, w[:, 0:8, :], wa_d, wa_d.t.rearrange("(kc p) n -> p kc n", p=128)[:, :, c0:c0 + 512])
                kb.dma("pool", w, w[:, 8:16, :], wb_d, wb_d.t.rearrange("(kc p) n -> p kc n", p=128)[:, :, c0:c0 + 512])
                kb.dma("pool", w, w[:, 16:32, :], wc_d, wc_d.t.rearrange("(kc p) n -> p kc n", p=128)[:, :, c0:c0 + 512])
                wb_[cg] = w
            t0, n, _ = TGS[ti]
            a, g = aring.next(), gring.next()
            kb.dma("sp", a, a[:, 0:8, 0:n], self.ZA, zav[:, :, t0:t0 + n])
            kb.dma("sp", a, a[:, 8:16, 0:n], self.ZB, zbv[:, :, t0:t0 + n])
            kb.dma("sp", a, a[:, 16:32, 0:n], self.ZC, zcv[:, :, t0:t0 + n])
            for s3 in range(3):
                q0 = QC_GM + s3 * 32 + cg * 4
                kb.dma("sp", g, g[:, s3, :, 0:n], self.Q, qv[:, q0:q0 + 4, t0:t0 + n])
            ab_[s] = (a, g)
        prefetch(0)
        for s, (cg, ti) in enumerate(steps):
            if s + 1 < len(steps):
                prefetch(s + 1)
            t0, n, _ = TGS[ti]
            w = wb_[cg]
            a, g = ab_.pop(s)
            st = stg.next()
            for j in range(4):
                csl = slice(j * 128, (j + 1) * 128)
                pss = []
                for (k0, k1) in ((0, 8), (8, 16), (16, 32)):
                    ps = kb.ps()
                    for kc in range(k0, k1):
                        kb.op("pe", lambda e: e.matmul(ps[:, 0:n], w[:, kc, csl], a[:, kc, 0:n], start=(kc == k0), stop=(kc == k1 - 1)),
                              [w, a], [ps])
                    pss.append(ps)
                sig = sgr.next()
                act(lambda e: e.activation(out=sig[:, :, 0:n], in_=g[:, :, j, 0:n], func=AF.Sigmoid), [g], [sig])
                dve(lambda e: e.tensor_tensor(t1[:, 0:n], pss[0][:, 0:n], sig[:, 0, 0:n], ALU.mult), [pss[0], sig], [t1])
                dve(lambda e: e.tensor_tensor(t2[:, 0:n], pss[1][:, 0:n], sig[:, 1, 0:n], ALU.mult), [pss[1], sig], [t2])
                dve(lambda e: e.tensor_tensor(t1[:, 0:n], t1[:, 0:n], t2[:, 0:n], ALU.add), [t2], [t1])
                dve(lambda e: e.tensor_tensor(t2[:, 0:n], pss[2][:, 0:n], sig[:, 2, 0:n], ALU.mult), [pss[2], sig], [t2])
                dve(lambda e: e.tensor_tensor(st[:, j, 0:n], t1[:, 0:n], t2[:, 0:n], ALU.add), [t1, t2], [st])
            kb.dma("sp", self.Y, self.Y.t.rearrange("c p t -> p c t")[:, cg * 4:cg * 4 + 4, t0:t0 + n], st, st[:, :, 0:n], nowaw=True)
        kb.end_phase()


def _layer_out(self):
    kb = self.kb
    wo_d = self.din("w_out", [D, D])
    self.OUT = self.scratch("OUT", [32, 128, T], F32)
    if self.last:
        self.xT_out = kb.dram("xT_out", [D, T], F32, "ExternalOutput")
    else:
        self.xT_out = self.scratch("XS%d" % (self.lidx % 2), [D, T], F32)
    with ExitStack() as ph:
        wring = Ring(kb, [128, 32, 512], BF16, 2, ph, "w")
        aring = Ring(kb, [128, 32, 512], BF16, 2, ph, "a")
        stg = Ring(kb, [128, 4, 512], F32, 2, ph, "stg")
        yv = self.Y.t.rearrange("c p t -> p c t")
        wv = wo_d.t.rearrange("(kc p) n -> p kc n", p=128)
        steps = [(cg, ti) for cg in range(8) for ti in range(len(TGS))]
        wb_, ab_ = {}, {}

        def prefetch(s):
            cg, ti = steps[s]
            if ti == 0:
                w = wring.next()
                kb.dma("pool", w, w[:], wo_d, wv[:, :, cg * 512:(cg + 1) * 512])
                wb_[cg] = w
            t0, n, _ = TGS[ti]
            a = aring.next()
            kb.dma("sp", a, a[:, :, 0:n], self.Y, yv[:, :, t0:t0 + n])
            ab_[s] = a
        prefetch(0)
        for s, (cg, ti) in enumerate(steps):
            if s + 1 < len(steps):
                prefetch(s + 1)
            t0, n, _ = TGS[ti]
            w, a = wb_[cg], ab_.pop(s)
            pss = [kb.ps() for _ in range(4)]
            for kc in range(32):
                for j in range(4):
                    kb.op("pe", lambda e: e.matmul(pss[j][:, 0:n], w[:, kc, j * 128:(j + 1) * 128], a[:, kc, 0:n],
                                                   start=(kc == 0), stop=(kc == 31)), [w, a], [pss[j]])
            st = stg.next()
            for j in range(4):
                self.copy_evac(st[:, j, 0:n], pss[j][:, 0:n], [pss[j]], [st])
            kb.dma("sp", self.OUT, self.OUT.t.rearrange("c p t -> p c t")[:, cg * 4:cg * 4 + 4, t0:t0 + n], st, st[:, :, 0:n],
                   nowaw=True)
        kb.end_phase()
    with ExitStack() as ph:
        sb = lambda shape, dt=F32, name="fn": kb.sbuf(shape, dt, name, ph)
        dve = lambda fn, r, w: kb.op("dve", fn, r, w)
        act = lambda fn, r, w: kb.op("act", fn, r, w)
        ob = Ring(kb, [128, 32, 512], F32, 2, ph, "ob")
        xr = Ring(kb, [128, 512], F32, 4, ph, "xr")
        sqr = Ring(kb, [128, 512], BF16, 3, ph, "sq")
        tr = Ring(kb, [128, 512], F32, 3, ph, "t")
        xo = Ring(kb, [128, 512], F32, 3, ph, "xo")
        rstd = sb([128, 512])
        ov = self.OUT.t.rearrange("c p t -> p c t")
        xv = self.xT.t.rearrange("(kc p) t -> p kc t", p=128)
        xov = self.xT_out.t.rearrange("(kc p) t -> p kc t", p=128)
        bufs = {}

        def load(i):
            t0, n, _ = TGS[i]
            b = ob.next()
            kb.dma("sp", b, b[:, :, 0:n], self.OUT, ov[:, :, t0:t0 + n])
            bufs[i] = b
        load(0)
        for i, (t0, n, v) in enumerate(TGS):
            if i + 1 < len(TGS):
                load(i + 1)
            o = bufs.pop(i)
            ss = kb.ps()
            for kc in range(32):
                sq = sqr.next()
                act(lambda e: e.activation(out=sq[:, 0:n], in_=o[:, kc, 0:n], func=AF.Square), [o], [sq])
                kb.op("pe", lambda e: e.matmul(ss[:, 0:n], self.ones[:], sq[:, 0:n], start=(kc == 0), stop=(kc == 31)),
                      [self.ones, sq], [ss])
            act(lambda e: e.activation(out=rstd[:, 0:n], in_=ss[:, 0:n], func=AF.Sqrt, scale=1.0 / D, bias=self.eps[:]),
                [ss, self.eps], [rstd])
            dve(lambda e: e.reciprocal(rstd[:, 0:n], rstd[:, 0:n]), [], [rstd])
            for kc in range(32):
                x = xr.next()
                kb.dma("sp", x, x[:, 0:n], self.xT, xv[:, kc, t0:t0 + n])
                t, y = tr.next(), xo.next()
                dve(lambda e: e.scalar_tensor_tensor(t[:, 0:n], o[:, kc, 0:n], self.gpg_s[:, kc, v:v + 1], rstd[:, 0:n],
                                                     ALU.mult, ALU.mult), [o, self.gpg_s, rstd], [t])
                kb.op("pool", lambda e: e.tensor_tensor(y[:, 0:n], t[:, 0:n], x[:, 0:n], ALU.add), [t, x], [y])
                kb.dma("sp", self.xT_out, xov[:, kc, t0:t0 + n], y, y[:, 0:n], nowaw=True)
        kb.end_phase()


Layer.p_s5fin = _layer_s5fin
Layer.p_merge = _layer_merge
Layer.p_out = _layer_out


def build_program(nlayers=DEPTH, dbg=()):
    L = None
    for l in range(nlayers):
        L = Layer(dbg=dbg, prev=L, lidx=l, last=(l == nlayers - 1))
        L.consts()
        L.p_mod()
        L.p_prenorm()
        L.p_proj()
        L.p_mla_proj()
        L.p_s5()
        L.p_s5fin()
        L.p_attn()
        L.p_merge()
        L.p_out()
    return L.kb.finish()


LAYER_KEYS = ("w_mod", "w_in", "w_uq", "w_ukv", "w_glu", "w_br_s5", "w_br_diff", "w_br_mla", "w_out")


def layer_inputs(inp, l, b, consts, single):
    sfx = "" if single else "_L%d" % l
    im = {}
    for k in LAYER_KEYS:
        im[k + sfx] = inp[k][l]
    im["bmod" + sfx] = pp_layout(inp["b_mod"][l])
    im["gpre" + sfx] = pp_layout(inp["g_pre"][l])
    im["gpost" + sfx] = pp_layout(inp["g_post"][l])
    im["gq" + sfx] = pp_layout(inp["g_q"][l])
    im["gkv" + sfx] = pp_layout(inp["g_kv"][l])
    im["dlam" + sfx] = np.ascontiguousarray(np.broadcast_to(inp["diff_lambda"][l][None], (128, 4, 64)))
    im["gdiff" + sfx] = pp_layout(inp["g_diff"][l])
    lam_init = 0.8 - 0.6 * math.exp(-0.3 * l)
    im["lam_init" + sfx] = np.ascontiguousarray(np.broadcast_to(np.array([lam_init, 1.0 - lam_init], np.float32)[None], (128, 2)))
    for k, v in host_s5(inp, l).items():
        im[k + sfx] = v
    return im


def kernel(**inputs):
    inp = {k: np.asarray(v) for k, v in inputs.items()}
    consts = host_consts()
    nc = build_program(DEPTH)
    in_maps = []
    for b in range(NB):
        im = dict(consts)
        im["xT"] = np.ascontiguousarray(np.concatenate([inp["ctx"][b], inp["x"][b]], axis=0).T.astype(np.float32))
        cv = np.stack([inp["c"][b], inp["c_ctx"]], axis=-1)
        im["cvec"] = np.ascontiguousarray(cv.reshape(32, 128, 2).transpose(1, 0, 2))
        for l in range(DEPTH):
            im.update(layer_inputs(inp, l, b, consts, False))
        in_maps.append(im)
    res = run_spmd(nc, in_maps)
    out = np.stack([res[b]["xT_out"].T[LC:] for b in range(NB)], axis=0)
    return np.ascontiguousarray(out.astype(np.float32))
```

```python
import math
import numpy as np
import ml_dtypes
from contextlib import ExitStack
import concourse.bass as bass
import concourse.mybir as mybir
from concourse.bass_utils import run_bass_kernel_spmd

F32 = mybir.dt.float32
BF16 = mybir.dt.bfloat16
AF = mybir.ActivationFunctionType
ALU = mybir.AluOpType
AX = mybir.AxisListType
NPBF = ml_dtypes.bfloat16

D = 4096
DEPTH = 4
NB = 2
SEQ = 4096
LC = 256
NTOK = SEQ + LC
KV_COLS = 3648
IN_COLS = 21824
EPS = 1e-6
SAME_ENGINE_WINDOW = 6


class Buf:
    __slots__ = ("t", "w", "r", "dsem", "dcnt", "name")

    def __init__(self, t, name=""):
        self.t = t
        self.w = None
        self.r = []
        self.dsem = None
        self.dcnt = 0
        self.name = name

    def __getitem__(self, idx):
        return self.t[idx]


class KB:
    def __init__(self):
        self.nc = bass.Bass("TRN2", target_bir_lowering=False)
        nc = self.nc
        self.es = ExitStack()
        self.eng = {"pe": nc.tensor, "act": nc.scalar, "dve": nc.vector, "pool": nc.gpsimd, "sp": nc.sync}
        self.clk = {}
        self.cnt = {}
        for e in ("pe", "act", "dve", "pool"):
            self.clk[e] = self.es.enter_context(nc.semaphore("clk_" + e))
            self.cnt[e] = 0
        self.waited = {e: {} for e in self.eng}
        self.dma_toks = {}
        self.n = 0
        self.psums = []
        self.ps_i = 0
        self.sem_free = []
        self.phase_bufs = []

    def uname(self, p):
        self.n += 1
        return "%s_%d" % (p, self.n)

    def sbuf(self, shape, dt, name="sb", es=None):
        t = (es or self.es).enter_context(self.nc.sbuf_tensor(self.uname(name), list(shape), dt))
        b = Buf(t, name)
        if es is not None:
            self.phase_bufs.append(b)
        return b

    def psum_pool(self, nbanks=8):
        for i in range(nbanks):
            t = self.es.enter_context(self.nc.psum_tensor(self.uname("ps"), [128, 512], F32))
            self.psums.append(Buf(t, "ps%d" % i))

    def ps(self):
        b = self.psums[self.ps_i % len(self.psums)]
        self.ps_i += 1
        return b

    def dram(self, name, shape, dt, kind):
        return Buf(self.nc.dram_tensor(name, list(shape), dt, kind=kind).ap(), name)

    def ensure(self, e, tok):
        if tok is None:
            return
        sem, val, src = tok
        if src == e:
            if e == "pe" or self.cnt[e] - val >= SAME_ENGINE_WINDOW:
                return
        k = id(sem)
        if self.waited[e].get(k, 0) >= val:
            return
        self.waited[e][k] = val
        self.eng[e].wait_ge(sem, val)

    def deps(self, e, reads, writes):
        for b in reads:
            self.ensure(e, b.w)
        for b in writes:
            self.ensure(e, b.w)
            for t in b.r:
                self.ensure(e, t)

    def op(self, e, fn, reads=(), writes=()):
        self.deps(e, reads, writes)
        ins = fn(self.eng[e])
        self.cnt[e] += 1
        ins.then_inc(self.clk[e], 1)
        tok = (self.clk[e], self.cnt[e], e)
        for b in writes:
            b.w = tok
            b.r = []
        for b in reads:
            if b.w is not tok:
                b.r.append(tok)
                if len(b.r) > 24:
                    b.r = self._compact(b.r)
        return tok

    @staticmethod
    def _compact(toks):
        best = {}
        for sem, val, src in toks:
            k = id(sem)
            if k not in best or best[k][1] < val:
                best[k] = (sem, val, src)
        return list(best.values())

    def dma(self, q, out_b, out_ap, in_b, in_ap, nowaw=False):
        if nowaw:
            self.deps(q, [in_b], [])
        else:
            self.deps(q, [in_b], [out_b])
        if out_b.dsem is None:
            if self.sem_free:
                out_b.dsem, out_b.dcnt = self.sem_free.pop()
            else:
                out_b.dsem = self.es.enter_context(self.nc.semaphore(self.uname("d")))
        ins = self.eng[q].dma_start(out=out_ap, in_=in_ap)
        out_b.dcnt += 16
        ins.then_inc(out_b.dsem, 16)
        tok = (out_b.dsem, out_b.dcnt, None)
        out_b.w = tok
        out_b.r = []
        in_b.r.append(tok)
        if len(in_b.r) > 24:
            in_b.r = self._compact(in_b.r)
        self.dma_toks[id(out_b.dsem)] = tok
        return tok

    def barrier(self):
        for e in self.eng:
            for f in self.clk:
                if f != e and self.cnt[f] > 0:
                    self.ensure(e, (self.clk[f], self.cnt[f], f))
            for tok in self.dma_toks.values():
                self.ensure(e, tok)

    def end_phase(self):
        self.barrier()
        for b in self.phase_bufs:
            if b.dsem is not None:
                self.sem_free.append((b.dsem, b.dcnt))
                b.dsem = None
        self.phase_bufs = []

    def finish(self):
        self.barrier()
        self.es.close()
        return self.nc


def run_spmd(nc, in_maps):
    res = run_bass_kernel_spmd(nc, in_maps, core_ids=list(range(len(in_maps))))
    return res.results


T = NTOK
TGS = [(0, 256, 1)] + [(256 + 512 * i, 512, 0) for i in range(8)]
NQC = 142
QC_DQ, QC_CQ, QC_GA, QC_GB, QC_GC, QC_GM = 0, 8, 14, 22, 30, 46


class Ring:
    def __init__(self, kb, shape, dt, n, es, name):
        self.bufs = [kb.sbuf(shape, dt, name, es) for _ in range(n)]
        self.i = 0

    def next(self):
        b = self.bufs[self.i % len(self.bufs)]
        self.i += 1
        return b


SHARED_INPUTS = ("pmat", "identb", "onesb", "jmat", "ropeC", "ropeS", "cvec")


class Layer:
    def __init__(self, dbg=(), phases=None, prev=None, lidx=0, last=True):
        self.dbg = set(dbg)
        self.phases = phases
        self.lidx = lidx
        self.last = last
        self.prev = prev
        if prev is None:
            self.kb = KB()
            self.kb.psum_pool(8)
            self.shared = {}
        else:
            self.kb = prev.kb
            self.shared = prev.shared
        self.evi = 0

    def din(self, name, shape, dt=F32):
        if name in SHARED_INPUTS:
            if name not in self.shared:
                self.shared[name] = self.kb.dram(name, shape, dt, "ExternalInput")
            return self.shared[name]
        if name == "xT":
            if self.prev is not None:
                return self.prev.xT_out
            return self.kb.dram(name, shape, dt, "ExternalInput")
        if self.prev is not None or not self.last:
            name = "%s_L%d" % (name, self.lidx)
        return self.kb.dram(name, shape, dt, "ExternalInput")

    def scratch(self, name, shape, dt=BF16):
        if name in self.shared:
            return self.shared[name]
        kind = "ExternalOutput" if name in self.dbg else "Internal"
        b = self.kb.dram(name, shape, dt, kind)
        self.shared[name] = b
        return b

    def on(self, p):
        return self.phases is None or p in self.phases

    def copy_evac(self, out_ap, in_ap, reads, writes):
        self.evi += 1
        if self.evi % 2:
            return self.kb.op("act", lambda e: e.activation(out=out_ap, in_=in_ap, func=AF.Copy), reads, writes)
        return self.kb.op("dve", lambda e: e.tensor_copy(out_ap, in_ap), reads, writes)

    def consts(self):
        kb = self.kb
        if self.prev is not None:
            for k in ("pmat", "ident", "ones", "jmat", "eps"):
                setattr(self, k, getattr(self.prev, k))
            return
        self.pmat_d = self.din("pmat", [128, 128], BF16)
        self.ident_d = self.din("identb", [128, 128], BF16)
        self.ones_d = self.din("onesb", [128, 128], BF16)
        self.jmat_d = self.din("jmat", [128, 128], BF16)
        self.pmat = kb.sbuf([128, 128], BF16, "pmat")
        self.ident = kb.sbuf([128, 128], BF16, "ident")
        self.ones = kb.sbuf([128, 128], BF16, "ones")
        self.jmat = kb.sbuf([128, 128], BF16, "jmat")
        for s, d in ((self.pmat, self.pmat_d), (self.ident, self.ident_d), (self.ones, self.ones_d),
                     (self.jmat, self.jmat_d)):
            kb.dma("sp", s, s[:], d, d[:, :])
        self.eps = kb.sbuf([128, 1], F32, "eps")
        kb.op("dve", lambda e: e.memset(self.eps[:], EPS), [], [self.eps])

    def p_mod(self):
        kb = self.kb
        cvec = self.din("cvec", [128, 32, 2])
        wmod = self.din("w_mod", [D, 3 * D])
        bmod = self.din("bmod", [128, 96])
        gpre = self.din("gpre", [128, 32])
        gpost = self.din("gpost", [128, 32])
        self.shift_s = kb.sbuf([128, 32, 2], F32, "shift")
        self.gs_s = kb.sbuf([128, 32, 2], F32, "gs")
        self.gpg_s = kb.sbuf([128, 32, 2], F32, "gpg")
        with ExitStack() as ph:
            cv_f = kb.sbuf([128, 32, 2], F32, "cvf", ph)
            cv_b = kb.sbuf([128, 32, 2], BF16, "cvb", ph)
            bm = kb.sbuf([128, 96], F32, "bm", ph)
            gp = kb.sbuf([128, 32], F32, "gp", ph)
            gq = kb.sbuf([128, 32], F32, "gq", ph)
            kb.dma("sp", cv_f, cv_f[:], cvec, cvec[:, :, :])
            kb.dma("sp", bm, bm[:], bmod, bmod[:, :])
            kb.dma("sp", gp, gp[:], gpre, gpre[:, :])
            kb.dma("sp", gq, gq[:], gpost, gpost[:, :])
            kb.op("act", lambda e: e.activation(out=cv_b[:], in_=cv_f[:], func=AF.Silu), [cv_f], [cv_b])
            ring = Ring(kb, [128, 32, 512], BF16, 2, ph, "wmod")
            wv = wmod.t.rearrange("(kc p) n -> p kc n", p=128)
            ws = {}

            def load(g):
                w = ring.next()
                kb.dma("pool", w, w[:], wmod, wv[:, :, g * 512:(g + 1) * 512])
                ws[g] = w
            load(0)
            for g in range(24):
                if g + 1 < 24:
                    load(g + 1)
                w = ws.pop(g)
                for q in range(4):
                    j = g * 4 + q
                    s, kf = j // 32, j % 32
                    ps = kb.ps()
                    for kc in range(32):
                        kb.op("pe", lambda e, kc=kc: e.matmul(ps[:, 0:2], w[:, kc, q * 128:(q + 1) * 128],
                                                             cv_b[:, kc, :], start=(kc == 0), stop=(kc == 31)),
                              [w, cv_b], [ps])
                    if s == 0:
                        kb.op("dve", lambda e: e.tensor_scalar(self.shift_s[:, kf, :], ps[:, 0:2], bm[:, j:j + 1],
                                                               None, ALU.add), [ps, bm], [self.shift_s])
                    elif s == 1:
                        kb.op("dve", lambda e: e.tensor_scalar(self.gs_s[:, kf, :], ps[:, 0:2], bm[:, j:j + 1], 1.0,
                                                               ALU.add, ALU.add), [ps, bm], [self.gs_s])
                        kb.op("dve", lambda e: e.tensor_scalar(self.gs_s[:, kf, :], self.gs_s[:, kf, :],
                                                               gp[:, kf:kf + 1], None, ALU.mult),
                              [gp], [self.gs_s])
                    else:
                        kb.op("dve", lambda e: e.tensor_scalar(self.gpg_s[:, kf, :], ps[:, 0:2], bm[:, j:j + 1],
                                                               gq[:, kf:kf + 1], ALU.add, ALU.mult),
                              [ps, bm, gq], [self.gpg_s])
            kb.end_phase()

    def p_prenorm(self):
        kb = self.kb
        self.xT = self.din("xT", [D, T])
        self.HT = self.scratch("HT", [32, 128, T])
        xv = self.xT.t.rearrange("(kc p) t -> p kc t", p=128)
        hv = self.HT.t.rearrange("c p t -> p c t")
        with ExitStack() as ph:
            xr = Ring(kb, [128, 32, 512], F32, 2, ph, "xb")
            hr = Ring(kb, [128, 32, 512], BF16, 2, ph, "hb")
            sqr = Ring(kb, [128, 512], BF16, 3, ph, "sq")
            tr = Ring(kb, [128, 512], F32, 3, ph, "t32")
            rstd = kb.sbuf([128, 512], F32, "rstd", ph)
            xs = {}

            def load(i):
                t0, n, v = TGS[i]
                b = xr.next()
                kb.dma("sp", b, b[:, :, 0:n], self.xT, xv[:, :, t0:t0 + n])
                xs[i] = b
            load(0)
            for i, (t0, n, v) in enumerate(TGS):
                if i + 1 < len(TGS):
                    load(i + 1)
                xb = xs.pop(i)
                ss = kb.ps()
                for kc in range(32):
                    sq = sqr.next()
                    kb.op("act", lambda e: e.activation(out=sq[:, 0:n], in_=xb[:, kc, 0:n], func=AF.Square),
                          [xb], [sq])
                    kb.op("pe", lambda e: e.matmul(ss[:, 0:n], self.ones[:], sq[:, 0:n], start=(kc == 0),
                                                   stop=(kc == 31)), [self.ones, sq], [ss])
                kb.op("act", lambda e: e.activation(out=rstd[:, 0:n], in_=ss[:, 0:n], func=AF.Sqrt,
                                                    scale=1.0 / D, bias=self.eps[:]), [ss, self.eps], [rstd])
                kb.op("dve", lambda e: e.reciprocal(rstd[:, 0:n], rstd[:, 0:n]), [], [rstd])
                hb = hr.next()
                for kc in range(32):
                    t32 = tr.next()
                    kb.op("dve", lambda e: e.scalar_tensor_tensor(t32[:, 0:n], xb[:, kc, 0:n],
                                                                  self.gs_s[:, kc, v:v + 1], rstd[:, 0:n],
                                                                  ALU.mult, ALU.mult),
                          [xb, self.gs_s, rstd], [t32])
                    kb.op("act", lambda e: e.activation(out=hb[:, kc, 0:n], in_=t32[:, 0:n], func=AF.Identity,
                                                        bias=self.shift_s[:, kc, v:v + 1], scale=1.0),
                          [t32, self.shift_s], [hb])
                kb.dma("sp", self.HT, hv[:, :, t0:t0 + n], hb, hb[:, :, 0:n], nowaw=True)
            kb.end_phase()


def pp_layout(v):
    return np.ascontiguousarray(np.asarray(v).reshape(-1, 128).T)


def host_consts():
    idx = np.arange(128)
    pmat = np.zeros((128, 128), np.float32)
    pmat[idx ^ 16, idx] = 1.0
    jm = np.zeros((128, 128), np.float32)
    jm[idx, 127 - idx] = 1.0
    r = np.arange(128)
    d = r % 64
    a, b, f = d // 32, (d // 16) % 2, d % 16
    inv = (np.float32(10000.0) ** (-np.arange(16, dtype=np.float32) / np.float32(16))).astype(np.float32)
    pos = np.arange(SEQ)
    prow, pcol = (pos // 64).astype(np.float32), (pos % 64).astype(np.float32)
    p2 = np.where(a[:, None] == 0, prow[None, :], pcol[None, :]).astype(np.float32)
    ang = (p2 * inv[f][:, None]).astype(np.float32)
    C = np.ones((128, T), np.float32)
    S = np.zeros((128, T), np.float32)
    C[:, LC:] = np.cos(ang)
    S[:, LC:] = np.sin(ang) * np.where(b == 0, -1.0, 1.0)[:, None].astype(np.float32)
    return {"pmat": pmat.astype(NPBF), "identb": np.eye(128, dtype=np.float32).astype(NPBF),
            "onesb": np.ones((128, 128), np.float32).astype(NPBF), "jmat": jm.astype(NPBF),
            "ropeC": C, "ropeS": S}


def _layer_p1(self):
    kb = self.kb
    self.w_in = self.din("w_in", [D, IN_COLS])
    ropeC_d = self.din("ropeC", [128, T])
    ropeS_d = self.din("ropeS", [128, T])
    self.kb_in = {"ropeC": ropeC_d, "ropeS": ropeS_d}
    self.U = self.scratch("U", [T, 1024])
    self.DV = self.scratch("DV", [T, 1024])
    self.DK = self.scratch("DK", [8, 128, T])
    self.CKV = self.scratch("CKV", [4, 128, T])
    self.KR = self.scratch("KR", [64, T])
    self.Q = self.scratch("Q", [NQC, 128, T])
    wv = self.w_in.t.rearrange("(kc p) n -> p kc n", p=128)
    hv = self.HT.t.rearrange("c p t -> p c t")
    groups = [("u", 0, 512), ("u", 512, 512), ("dk", 1024, 512), ("dk", 1536, 512), ("dv", 2048, 512),
              ("dv", 2560, 512), ("ckv", 3072, 512), ("kr", 3584, 64)]
    groups += [("q", KV_COLS + 512 * i, 512) for i in range(35)] + [("q", KV_COLS + 512 * 35, 256)]
    if self.phases is not None and "p1_small" in self.phases:
        groups = groups[:9]
    import os
    if os.environ.get("P1_GROUPS"):
        groups = [groups[int(i)] for i in os.environ["P1_GROUPS"].split(",")]
    with ExitStack() as ph:
        self.ropeC = kb.sbuf([128, T], F32, "ropeC", ph)
        self.ropeS = kb.sbuf([128, T], F32, "ropeS", ph)
        kb.dma("sp", self.ropeC, self.ropeC[:], ropeC_d, ropeC_d[:, :])
        kb.dma("sp", self.ropeS, self.ropeS[:], ropeS_d, ropeS_d[:, :])
        wring = Ring(kb, [128, 32, 512], BF16, 2, ph, "w")
        aring = Ring(kb, [128, 32, 512], BF16, 2, ph, "a")
        stg = Ring(kb, [128, 4, 512], BF16, 3, ph, "stg")
        rawb = Ring(kb, [128, 512], BF16, 2, ph, "rawb")
        t1r = Ring(kb, [128, 512], F32, 2, ph, "t1")
        t2r = Ring(kb, [128, 512], F32, 2, ph, "t2")
        steps = [(gi, ti) for gi in range(len(groups)) for ti in range(len(TGS))]
        wbuf, abuf = {}, {}

        def prefetch(s):
            gi, ti = steps[s]
            kind, c0, nc_ = groups[gi]
            if ti == 0:
                w = wring.next()
                kb.dma("pool", w, w[:, :, 0:nc_], self.w_in, wv[:, :, c0:c0 + nc_])
                wbuf[gi] = w
            t0, n, v = TGS[ti]
            a = aring.next()
            kb.dma("sp", a, a[:, :, 0:n], self.HT, hv[:, :, t0:t0 + n])
            abuf[s] = a
        prefetch(0)
        for s, (gi, ti) in enumerate(steps):
            if s + 1 < len(steps):
                prefetch(s + 1)
            kind, c0, nc_ = groups[gi]
            t0, n, v = TGS[ti]
            w = wbuf[gi]
            a = abuf.pop(s)
            if kind in ("u", "dv"):
                dst = self.U if kind == "u" else self.DV
                dc0 = c0 if kind == "u" else c0 - 2048
                st = stg.next()
                for tt in range(n // 128):
                    ps = kb.ps()
                    for kc in range(32):
                        kb.op("pe", lambda e: e.matmul(ps[:, 0:nc_], a[:, kc, tt * 128:(tt + 1) * 128], w[:, kc, 0:nc_],
                                                       start=(kc == 0), stop=(kc == 31)), [a, w], [ps])
                    self.copy_evac(st[:, tt, 0:nc_], ps[:, 0:nc_], [ps], [st])
                kb.dma("sp", dst, dst.t[t0:t0 + n, dc0:dc0 + nc_].rearrange("(tt p) c -> p tt c", p=128),
                       st, st[:, 0:n // 128, 0:nc_], nowaw=True)
                continue
            nch = (nc_ + 127) // 128
            M = min(128, nc_)
            pss = [kb.ps() for _ in range(nch)]
            for kc in range(32):
                for j in range(nch):
                    kb.op("pe", lambda e: e.matmul(pss[j][0:M, 0:n], w[:, kc, j * 128:j * 128 + M], a[:, kc, 0:n],
                                                   start=(kc == 0), stop=(kc == 31)), [w, a], [pss[j]])
            st = stg.next()
            qc0 = (c0 - KV_COLS) // 128
            rope = kind in ("dk", "kr") or (kind == "q" and qc0 < QC_CQ)
            for j in range(nch):
                ps = pss[j]
                if not rope:
                    self.copy_evac(st[0:M, j, 0:n], ps[0:M, 0:n], [ps], [st])
                    continue
                rb, t1, t2 = rawb.next(), t1r.next(), t2r.next()
                kb.op("dve", lambda e: e.tensor_copy(rb[0:M, 0:n], ps[0:M, 0:n]), [ps], [rb])
                pp = kb.ps()
                kb.op("pe", lambda e: e.matmul(pp[0:M, 0:n], self.pmat[0:M, 0:M], rb[0:M, 0:n], start=True, stop=True),
                      [self.pmat, rb], [pp])
                kb.op("dve", lambda e: e.tensor_tensor(t1[0:M, 0:n], ps[0:M, 0:n], self.ropeC[0:M, t0:t0 + n], ALU.mult),
                      [ps, self.ropeC], [t1])
                kb.op("dve", lambda e: e.tensor_tensor(t2[0:M, 0:n], pp[0:M, 0:n], self.ropeS[0:M, t0:t0 + n], ALU.mult),
                      [pp, self.ropeS], [t2])
                kb.op("dve", lambda e: e.tensor_tensor(st[0:M, j, 0:n], t1[0:M, 0:n], t2[0:M, 0:n], ALU.add),
                      [t1, t2], [st])
            if kind == "kr":
                kb.dma("sp", self.KR, self.KR.t[:, t0:t0 + n], st, st[0:64, 0, 0:n], nowaw=True)
            else:
                if kind == "dk":
                    dst, ch0 = self.DK, (c0 - 1024) // 128
                elif kind == "ckv":
                    dst, ch0 = self.CKV, 0
                else:
                    dst, ch0 = self.Q, qc0
                kb.dma("sp", dst, dst.t.rearrange("c p t -> p c t")[:, ch0:ch0 + nch, t0:t0 + n],
                       st, st[:, 0:nch, 0:n], nowaw=True)
        kb.end_phase()


Layer.p_proj = _layer_p1


def _layer_p1b(self):
    kb = self.kb
    wuq_d = self.din("w_uq", [768, 3072])
    wukv_d = self.din("w_ukv", [512, 4096])
    gq_d = self.din("gq", [128, 6])
    gkv_d = self.din("gkv", [128, 4])
    self.QN = self.scratch("QN", [16, 128, T])
    self.QR = self.scratch("QR", [16, 64, T])
    self.KN = self.scratch("KN", [16, 128, T])
    self.MV = self.scratch("MV", [T, 2048])
    qv = self.Q.t.rearrange("c p t -> p c t")
    cv = self.CKV.t.rearrange("c p t -> p c t")
    with ExitStack() as ph:
        ropeC = kb.sbuf([64, T], F32, "ropeC", ph)
        ropeS = kb.sbuf([64, T], F32, "ropeS", ph)
        rc_d, rs_d = self.kb_in["ropeC"], self.kb_in["ropeS"]
        kb.dma("sp", ropeC, ropeC[:], rc_d, rc_d[0:64, :])
        kb.dma("sp", ropeS, ropeS[:], rs_d, rs_d[0:64, :])
        wuq = kb.sbuf([128, 6, 3072], BF16, "wuq", ph)
        wukv = kb.sbuf([128, 4, 4096], BF16, "wukv", ph)
        gq = kb.sbuf([128, 6], F32, "gq", ph)
        gkv = kb.sbuf([128, 4], F32, "gkv", ph)
        kb.dma("pool", wuq, wuq[:], wuq_d, wuq_d.t.rearrange("(kc p) n -> p kc n", p=128))
        kb.dma("pool", wukv, wukv[:], wukv_d, wukv_d.t.rearrange("(kc p) n -> p kc n", p=128))
        kb.dma("sp", gq, gq[:], gq_d, gq_d[:, :])
        kb.dma("sp", gkv, gkv[:], gkv_d, gkv_d[:, :])
        cqr = Ring(kb, [128, 6, 512], BF16, 2, ph, "cq")
        ckr = Ring(kb, [128, 4, 512], BF16, 2, ph, "ckv")
        cqn = kb.sbuf([128, 6, 512], BF16, "cqn", ph)
        ckn = kb.sbuf([128, 4, 512], BF16, "ckn", ph)
        sqr = Ring(kb, [128, 512], BF16, 3, ph, "sq")
        rstd = kb.sbuf([128, 512], F32, "rstd", ph)
        stg = Ring(kb, [128, 4, 512], BF16, 3, ph, "stg")
        stq = Ring(kb, [64, 4, 512], BF16, 2, ph, "stq")
        rawb = Ring(kb, [64, 512], BF16, 2, ph, "rawb")
        t1r = Ring(kb, [64, 512], F32, 2, ph, "t1")
        t2r = Ring(kb, [64, 512], F32, 2, ph, "t2")
        wv_v = wukv.t[:, :, :].rearrange("p k (h two d) -> p k h two d", two=2, d=128)
        bufs = {}

        def load(i):
            t0, n, v = TGS[i]
            a, b = cqr.next(), ckr.next()
            kb.dma("sp", a, a[:, :, 0:n], self.Q, qv[:, QC_CQ:QC_CQ + 6, t0:t0 + n])
            kb.dma("sp", b, b[:, :, 0:n], self.CKV, cv[:, :, t0:t0 + n])
            bufs[i] = (a, b)

        def norm(src, dst, g, KC, n):
            ss = kb.ps()
            for kc in range(KC):
                sq = sqr.next()
                kb.op("act", lambda e: e.activation(out=sq[:, 0:n], in_=src[:, kc, 0:n], func=AF.Square), [src], [sq])
                kb.op("pe", lambda e: e.matmul(ss[:, 0:n], self.ones[:], sq[:, 0:n], start=(kc == 0),
                                               stop=(kc == KC - 1)), [self.ones, sq], [ss])
            kb.op("act", lambda e: e.activation(out=rstd[:, 0:n], in_=ss[:, 0:n], func=AF.Sqrt,
                                                scale=1.0 / (KC * 128), bias=self.eps[:]), [ss, self.eps], [rstd])
            kb.op("dve", lambda e: e.reciprocal(rstd[:, 0:n], rstd[:, 0:n]), [], [rstd])
            for kc in range(KC):
                kb.op("dve", lambda e: e.scalar_tensor_tensor(dst[:, kc, 0:n], src[:, kc, 0:n], g[:, kc:kc + 1],
                                                              rstd[:, 0:n], ALU.mult, ALU.mult),
                      [src, g, rstd], [dst])
        load(0)
        for i, (t0, n, v) in enumerate(TGS):
            if i + 1 < len(TGS):
                load(i + 1)
            cq, ck = bufs.pop(i)
            norm(cq, cqn, gq, 6, n)
            norm(ck, ckn, gkv, 4, n)
            for h0 in range(0, 16, 4):
                st, sq4 = stg.next(), stq.next()
                for hh in range(4):
                    h = h0 + hh
                    ps = kb.ps()
                    for kc in range(6):
                        kb.op("pe", lambda e: e.matmul(ps[:, 0:n], wuq[:, kc, h * 192:h * 192 + 128], cqn[:, kc, 0:n],
                                                       start=(kc == 0), stop=(kc == 5)), [wuq, cqn], [ps])
                    self.copy_evac(st[:, hh, 0:n], ps[:, 0:n], [ps], [st])
                    pr = kb.ps()
                    for kc in range(6):
                        kb.op("pe", lambda e: e.matmul(pr[0:64, 0:n], wuq[:, kc, h * 192 + 128:h * 192 + 192],
                                                       cqn[:, kc, 0:n], start=(kc == 0), stop=(kc == 5)),
                              [wuq, cqn], [pr])
                    rb, t1, t2 = rawb.next(), t1r.next(), t2r.next()
                    kb.op("dve", lambda e: e.tensor_copy(rb[:, 0:n], pr[0:64, 0:n]), [pr], [rb])
                    pp = kb.ps()
                    kb.op("pe", lambda e: e.matmul(pp[0:64, 0:n], self.pmat[0:64, 0:64], rb[:, 0:n], start=True,
                                                   stop=True), [self.pmat, rb], [pp])
                    kb.op("dve", lambda e: e.tensor_tensor(t1[:, 0:n], pr[0:64, 0:n], ropeC[:, t0:t0 + n], ALU.mult),
                          [pr, ropeC], [t1])
                    kb.op("dve", lambda e: e.tensor_tensor(t2[:, 0:n], pp[0:64, 0:n], ropeS[:, t0:t0 + n], ALU.mult),
                          [pp, ropeS], [t2])
                    kb.op("dve", lambda e: e.tensor_tensor(sq4[:, hh, 0:n], t1[:, 0:n], t2[:, 0:n], ALU.add),
                          [t1, t2], [sq4])
                kb.dma("sp", self.QN, self.QN.t.rearrange("c p t -> p c t")[:, h0:h0 + 4, t0:t0 + n],
                       st, st[:, :, 0:n], nowaw=True)
                kb.dma("sp", self.QR, self.QR.t.rearrange("c p t -> p c t")[:, h0:h0 + 4, t0:t0 + n],
                       sq4, sq4[:, :, 0:n], nowaw=True)
            for h0 in range(0, 16, 4):
                st = stg.next()
                for hh in range(4):
                    h = h0 + hh
                    ps = kb.ps()
                    for kc in range(4):
                        kb.op("pe", lambda e: e.matmul(ps[:, 0:n], wukv[:, kc, h * 256:h * 256 + 128], ckn[:, kc, 0:n],
                                                       start=(kc == 0), stop=(kc == 3)), [wukv, ckn], [ps])
                    self.copy_evac(st[:, hh, 0:n], ps[:, 0:n], [ps], [st])
                kb.dma("sp", self.KN, self.KN.t.rearrange("c p t -> p c t")[:, h0:h0 + 4, t0:t0 + n],
                       st, st[:, :, 0:n], nowaw=True)
            for h0 in range(0, 16, 4):
                st = stg.next()
                for tt in range(n // 128):
                    ps = kb.ps()
                    for kc in range(4):
                        kb.op("pe", lambda e: e.matmul(ps[:, 0:512], ckn[:, kc, tt * 128:(tt + 1) * 128],
                                                       wv_v[:, kc, h0:h0 + 4, 1, :], start=(kc == 0), stop=(kc == 3)),
                              [ckn, wukv], [ps])
                    self.copy_evac(st[:, tt, :], ps[:, 0:512], [ps], [st])
                kb.dma("sp", self.MV, self.MV.t[t0:t0 + n, h0 * 128:h0 * 128 + 512].rearrange("(tt p) c -> p tt c", p=128),
                       st, st[:, 0:n // 128, :], nowaw=True)
        kb.end_phase()


Layer.p_mla_proj = _layer_p1b


def host_s5(inp, l):
    lre, lim, ldt = inp["s5_lambda_re"][l], inp["s5_lambda_im"][l], inp["s5_log_dt"][l]
    bre, bim, cre, cim, dd = inp["s5_b_re"][l], inp["s5_b_im"][l], inp["s5_c_re"][l], inp["s5_c_im"][l], inp["s5_d"][l]
    lam = np.zeros((128, 3, 64), np.float32)
    BT = np.zeros((64, 2, 128, 128), np.float32)
    CM = np.zeros((64, 2, 128, 128), np.float32)
    for cc in range(8):
        for gp in range(4):
            for d in range(2):
                s = (cc * 4 + gp) * 2 + d
                for gl in range(2):
                    g = 8 * cc + 2 * gp + gl
                    lam[gl * 64:(gl + 1) * 64, 0, s] = lre[d, g]
                    lam[gl * 64:(gl + 1) * 64, 1, s] = lim[d, g]
                    lam[gl * 64:(gl + 1) * 64, 2, s] = ldt[d, g]
                    c0 = (2 * gp + gl) * 16
                    BT[s, 0, c0:c0 + 16, gl * 64:(gl + 1) * 64] = bre[d, g].T
                    BT[s, 1, c0:c0 + 16, gl * 64:(gl + 1) * 64] = bim[d, g].T
                    CM[s, 0, gl * 64:(gl + 1) * 64, c0:c0 + 16] = cre[d, g].T
                    CM[s, 1, gl * 64:(gl + 1) * 64, c0:c0 + 16] = cim[d, g].T
    DG = np.zeros((8, 128, 128), np.float32)
    dflat = dd.reshape(8, 128)
    for cc in range(8):
        DG[cc][np.arange(128), np.arange(128)] = dflat[cc]
    return {"s5_lam": lam, "s5_BT": BT, "s5_CM": CM, "s5_DG": DG}


def _layer_s5(self):
    kb = self.kb
    lam_d = self.din("s5_lam", [128, 3, 64])
    BT_d = self.din("s5_BT", [64, 2, 128, 128])
    CM_d = self.din("s5_CM", [64, 2, 128, 128])
    DG_d = self.din("s5_DG", [8, 128, 128])
    self.YA = self.scratch("YA", [T, 1024])
    TC = 512
    chunks = [(i * TC, min(TC, T - i * TC)) for i in range((T + TC - 1) // TC)]
    TWO_PI = 2.0 * math.pi
    ccs = range(8) if not (self.phases and "s5_small" in self.phases) else range(1)
    with ExitStack() as ph:
        sb = lambda shape, dt=F32, name="s5": kb.sbuf(shape, dt, name, ph)
        lam = sb([128, 3, 64])
        kb.dma("sp", lam, lam[:], lam_d, lam_d[:, :, :])
        V = {k: sb([128, 64]) for k in ("dt", "lr", "er", "th", "k", "s4", "c4", "s2", "c2", "st", "ct", "are", "aim",
                                        "nr", "den", "fr", "fi", "tmp", "tmp2")}
        ki = sb([128, 64], mybir.dt.int32)
        dve = lambda fn, r, w: kb.op("dve", fn, r, w)
        act = lambda fn, r, w: kb.op("act", fn, r, w)
        A = lambda k: V[k][:]
        act(lambda e: e.activation(out=A("dt"), in_=lam[:, 2, :], func=AF.Exp), [lam], [V["dt"]])
        dve(lambda e: e.tensor_scalar(A("lr"), lam[:, 0, :], -1e-4, None, ALU.min), [lam], [V["lr"]])
        dve(lambda e: e.tensor_tensor(A("tmp"), A("lr"), A("dt"), ALU.mult), [V["lr"], V["dt"]], [V["tmp"]])
        act(lambda e: e.activation(out=A("er"), in_=A("tmp"), func=AF.Exp), [V["tmp"]], [V["er"]])
        dve(lambda e: e.tensor_tensor(A("th"), lam[:, 1, :], A("dt"), ALU.mult), [lam, V["dt"]], [V["th"]])
        dve(lambda e: e.tensor_scalar(A("tmp"), A("th"), 1.0 / TWO_PI, None, ALU.mult), [V["th"]], [V["tmp"]])
        dve(lambda e: e.tensor_copy(ki[:], A("tmp")), [V["tmp"]], [ki])
        dve(lambda e: e.tensor_copy(A("k"), ki[:]), [ki], [V["k"]])
        dve(lambda e: e.scalar_tensor_tensor(A("tmp"), A("k"), -TWO_PI, A("th"), ALU.mult, ALU.add),
            [V["k"], V["th"]], [V["tmp"]])
        act(lambda e: e.activation(out=A("s4"), in_=A("tmp"), func=AF.Sin, scale=0.25), [V["tmp"]], [V["s4"]])
        dve(lambda e: e.tensor_tensor(A("tmp2"), A("s4"), A("s4"), ALU.mult), [V["s4"]], [V["tmp2"]])
        dve(lambda e: e.tensor_scalar(A("c2"), A("tmp2"), -2.0, 1.0, ALU.mult, ALU.add), [V["tmp2"]], [V["c2"]])
        dve(lambda e: e.tensor_scalar(A("tmp2"), A("tmp2"), -1.0, 1.0, ALU.mult, ALU.add), [], [V["tmp2"]])
        act(lambda e: e.activation(out=A("c4"), in_=A("tmp2"), func=AF.Sqrt), [V["tmp2"]], [V["c4"]])
        dve(lambda e: e.scalar_tensor_tensor(A("s2"), A("s4"), 2.0, A("c4"), ALU.mult, ALU.mult),
            [V["s4"], V["c4"]], [V["s2"]])
        dve(lambda e: e.scalar_tensor_tensor(A("st"), A("s2"), 2.0, A("c2"), ALU.mult, ALU.mult),
            [V["s2"], V["c2"]], [V["st"]])
        dve(lambda e: e.tensor_tensor(A("tmp"), A("s2"), A("s2"), ALU.mult), [V["s2"]], [V["tmp"]])
        dve(lambda e: e.tensor_scalar(A("ct"), A("tmp"), -2.0, 1.0, ALU.mult, ALU.add), [V["tmp"]], [V["ct"]])
        dve(lambda e: e.tensor_tensor(A("are"), A("er"), A("ct"), ALU.mult), [V["er"], V["ct"]], [V["are"]])
        dve(lambda e: e.tensor_tensor(A("aim"), A("er"), A("st"), ALU.mult), [V["er"], V["st"]], [V["aim"]])
        dve(lambda e: e.tensor_scalar(A("nr"), A("are"), -1.0, None, ALU.add), [V["are"]], [V["nr"]])
        dve(lambda e: e.tensor_tensor(A("den"), A("lr"), A("lr"), ALU.mult), [V["lr"]], [V["den"]])
        dve(lambda e: e.tensor_tensor(A("tmp"), lam[:, 1, :], lam[:, 1, :], ALU.mult), [lam], [V["tmp"]])
        dve(lambda e: e.tensor_tensor(A("den"), A("den"), A("tmp"), ALU.add), [V["tmp"]], [V["den"]])
        dve(lambda e: e.reciprocal(A("den"), A("den")), [], [V["den"]])
        dve(lambda e: e.tensor_tensor(A("tmp"), A("nr"), A("lr"), ALU.mult), [V["nr"], V["lr"]], [V["tmp"]])
        dve(lambda e: e.tensor_tensor(A("tmp2"), A("aim"), lam[:, 1, :], ALU.mult), [V["aim"], lam], [V["tmp2"]])
        dve(lambda e: e.tensor_tensor(A("tmp"), A("tmp"), A("tmp2"), ALU.add), [V["tmp2"]], [V["tmp"]])
        dve(lambda e: e.tensor_tensor(A("fr"), A("tmp"), A("den"), ALU.mult), [V["tmp"], V["den"]], [V["fr"]])
        dve(lambda e: e.tensor_tensor(A("tmp"), A("aim"), A("lr"), ALU.mult), [V["aim"], V["lr"]], [V["tmp"]])
        dve(lambda e: e.tensor_tensor(A("tmp2"), A("nr"), lam[:, 1, :], ALU.mult), [V["nr"], lam], [V["tmp2"]])
        dve(lambda e: e.tensor_tensor(A("tmp"), A("tmp"), A("tmp2"), ALU.subtract), [V["tmp2"]], [V["tmp"]])
        dve(lambda e: e.tensor_tensor(A("fi"), A("tmp"), A("den"), ALU.mult), [V["tmp"], V["den"]], [V["fi"]])

        Pc = sb([128, 10, 64], name="Pc")
        Ps = sb([128, 10, 64], name="Ps")
        dve(lambda e: e.tensor_copy(Pc[:, 0, :], A("ct")), [V["ct"]], [Pc])
        dve(lambda e: e.tensor_copy(Ps[:, 0, :], A("st")), [V["st"]], [Ps])
        for k in range(1, 10):
            dve(lambda e: e.tensor_tensor(A("tmp"), Ps[:, k - 1, :], Ps[:, k - 1, :], ALU.mult), [Ps], [V["tmp"]])
            dve(lambda e: e.tensor_tensor(A("tmp2"), Pc[:, k - 1, :], Pc[:, k - 1, :], ALU.mult), [Pc], [V["tmp2"]])
            dve(lambda e: e.tensor_tensor(Pc[:, k, :], A("tmp2"), A("tmp"), ALU.subtract), [V["tmp"], V["tmp2"]], [Pc])
            dve(lambda e: e.scalar_tensor_tensor(Ps[:, k, :], Pc[:, k - 1, :], 2.0, Ps[:, k - 1, :], ALU.mult, ALU.mult),
                [], [Ps])
        onesf = sb([128, TC])
        dve(lambda e: e.memset(onesf[:], 1.0), [], [onesf])
        Ec = [sb([128, TC], name="Ec") for _ in range(8)]
        Es = [sb([128, TC], name="Es") for _ in range(8)]
        Dc = [sb([128, TC], name="Dc") for _ in range(8)]
        Ds = [sb([128, TC], name="Ds") for _ in range(8)]
        Rt = [sb([128, TC], name="Rt") for _ in range(8)]
        nEs = [sb([128, TC], name="nEs") for _ in range(8)]
        ETC = [sb([128, 2], name="ETC") for _ in range(8)]
        em = sb([128, 4])
        ttab = sb([128, TC])
        BTs = [sb([128, 2, 128], BF16, "BT") for _ in range(8)]
        CMs = [sb([128, 2, 128], BF16, "CM") for _ in range(8)]
        DG = sb([128, 128], BF16, "DG")
        useq = [sb([128, T], BF16, "useq") for _ in range(2)]
        yR = sb([128, 34, 128], BF16, "yR")
        utile = Ring(kb, [128, 128], BF16, 3, ph, "utile")
        wr = Ring(kb, [128, TC], F32, 2, ph, "wr")
        wi = Ring(kb, [128, TC], F32, 2, ph, "wi")
        zr = Ring(kb, [128, TC], F32, 2, ph, "zr")
        zi = Ring(kb, [128, TC], F32, 2, ph, "zi")
        ta = Ring(kb, [128, TC], F32, 3, ph, "ta")
        tb = Ring(kb, [128, TC], F32, 3, ph, "tb")
        pa = Ring(kb, [128, TC], F32, 2, ph, "pa")
        pb = Ring(kb, [128, TC], F32, 2, ph, "pb")
        xr = Ring(kb, [128, TC], BF16, 8, ph, "xr")
        xi = Ring(kb, [128, TC], BF16, 8, ph, "xi")
        car = [sb([128, 2], name="car") for _ in range(4)]
        yst = Ring(kb, [128, 4, 128], BF16, 2, ph, "yst")
        for cc in ccs:
            kb.dma("pool", DG, DG[:], DG_d, DG_d[cc])
            for q in range(8):
                s = cc * 8 + q
                kb.dma("pool", BTs[q], BTs[q][:], BT_d, BT_d.t[s].rearrange("r c p -> c r p"))
                kb.dma("pool", CMs[q], CMs[q][:], CM_d, CM_d.t[s].rearrange("r p c -> p r c"))
                col = lambda k: V[k][:, s:s + 1]
                dve(lambda e: e.memset(Ec[q][:, 0:1], 1.0), [], [Ec[q]])
                dve(lambda e: e.memset(Es[q][:, 0:1], 0.0), [], [Es[q]])
                for k in range(9):
                    m = 1 << k
                    emc, ems = Pc[:, k, s:s + 1], Ps[:, k, s:s + 1]
                    dve(lambda e: e.tensor_scalar(ttab[:, 0:m], Es[q][:, 0:m], ems, None, ALU.mult), [Es[q], Ps], [ttab])
                    dve(lambda e: e.scalar_tensor_tensor(Ec[q][:, m:2 * m], Ec[q][:, 0:m], emc, ttab[:, 0:m],
                                                         ALU.mult, ALU.subtract), [ttab, Pc], [Ec[q]])
                    dve(lambda e: e.tensor_scalar(ttab[:, 0:m], Es[q][:, 0:m], emc, None, ALU.mult), [Es[q], Pc], [ttab])
                    dve(lambda e: e.scalar_tensor_tensor(Es[q][:, m:2 * m], Ec[q][:, 0:m], ems, ttab[:, 0:m],
                                                         ALU.mult, ALU.add), [ttab, Ps, Ec[q]], [Es[q]])
                dve(lambda e: e.tensor_copy(ETC[q][:, 0:1], Pc[:, 9, s:s + 1]), [Pc], [ETC[q]])
                dve(lambda e: e.tensor_copy(ETC[q][:, 1:2], Ps[:, 9, s:s + 1]), [Ps], [ETC[q]])
                dve(lambda e: e.tensor_scalar(ttab[:], Es[q][:], col("fi"), None, ALU.mult), [Es[q], V["fi"]], [ttab])
                dve(lambda e: e.scalar_tensor_tensor(Dc[q][:], Ec[q][:], col("fr"), ttab[:], ALU.mult, ALU.add),
                    [Ec[q], V["fr"], ttab], [Dc[q]])
                dve(lambda e: e.tensor_scalar(ttab[:], Es[q][:], col("fr"), None, ALU.mult), [Es[q], V["fr"]], [ttab])
                dve(lambda e: e.scalar_tensor_tensor(Ds[q][:], Ec[q][:], col("fi"), ttab[:], ALU.mult, ALU.subtract),
                    [Ec[q], V["fi"], ttab], [Ds[q]])
                dve(lambda e: e.tensor_scalar(Rt[q][:], onesf[:], col("er"), None, ALU.mult), [onesf, V["er"]], [Rt[q]])
                dve(lambda e: e.tensor_scalar(nEs[q][:], Es[q][:], -1.0, None, ALU.mult), [Es[q]], [nEs[q]])
            if "S5DBG" in self.dbg and cc == 0:
                dbg1 = kb.dram("S5DBG", [128, 19, 64], F32, "ExternalOutput")
                for ki_, k_ in enumerate(sorted(V)):
                    kb.dma("sp", dbg1, dbg1.t[:, ki_, :], V[k_], V[k_][:], nowaw=True)
                dbg2 = kb.dram("S5TAB", [128, 4, TC], F32, "ExternalOutput")
                for ki_, tb_ in enumerate((Ec[0], Es[0], Dc[0], Ds[0])):
                    kb.dma("sp", dbg2, dbg2.t[:, ki_, :], tb_, tb_[:], nowaw=True)
            for d in range(2):
                for r in range(34):
                    src = r if d == 0 else (1 - r if r < 2 else 35 - r)
                    ut = utile.next()
                    kb.dma("sp", ut, ut[:], self.U, self.U.t[src * 128:(src + 1) * 128, cc * 128:(cc + 1) * 128])
                    ps = kb.ps()
                    kb.op("pe", lambda e: e.matmul(ps[:, 0:128], ut[:], (self.ident if d == 0 else self.jmat)[:],
                                                   start=True, stop=True), [ut, self.ident, self.jmat], [ps])
                    self.copy_evac(useq[d][:, r * 128:(r + 1) * 128], ps[:, 0:128], [ps], [useq[d]])
            for d in (1, 0):
                for ci, (c0, n) in enumerate(chunks):
                    ntt = n // 128
                    xs = []
                    for gp in range(4):
                        q = gp * 2 + d
                        pbr, pbi = kb.ps(), kb.ps()
                        kb.op("pe", lambda e: e.matmul(pbr[:, 0:n], BTs[q][:, 0, :], useq[d][:, c0:c0 + n], start=True, stop=True),
                              [BTs[q], useq[d]], [pbr])
                        kb.op("pe", lambda e: e.matmul(pbi[:, 0:n], BTs[q][:, 1, :], useq[d][:, c0:c0 + n], start=True, stop=True),
                              [BTs[q], useq[d]], [pbi])
                        t1, t2, w_r, w_i = ta.next(), tb.next(), wr.next(), wi.next()
                        dve(lambda e: e.tensor_tensor(t1[:, 0:n], pbr[:, 0:n], Dc[q][:, 0:n], ALU.mult), [pbr, Dc[q]], [t1])
                        dve(lambda e: e.tensor_tensor(t2[:, 0:n], pbi[:, 0:n], Ds[q][:, 0:n], ALU.mult), [pbi, Ds[q]], [t2])
                        dve(lambda e: e.tensor_tensor(w_r[:, 0:n], t1[:, 0:n], t2[:, 0:n], ALU.subtract), [t1, t2], [w_r])
                        dve(lambda e: e.tensor_tensor(t1[:, 0:n], pbr[:, 0:n], Ds[q][:, 0:n], ALU.mult), [pbr, Ds[q]], [t1])
                        dve(lambda e: e.tensor_tensor(t2[:, 0:n], pbi[:, 0:n], Dc[q][:, 0:n], ALU.mult), [pbi, Dc[q]], [t2])
                        dve(lambda e: e.tensor_tensor(w_i[:, 0:n], t1[:, 0:n], t2[:, 0:n], ALU.add), [t1, t2], [w_i])
                        z_r, z_i = zr.next(), zi.next()
                        if ci == 0:
                            i0r, i0i, ib = 0.0, 0.0, []
                        else:
                            i0r, i0i, ib = car[gp][:, 0:1], car[gp][:, 1:2], [car[gp]]
                        dve(lambda e: e.tensor_tensor_scan(z_r[:, 0:n], Rt[q][:, 0:n], w_r[:, 0:n], i0r, ALU.mult, ALU.add),
                            [Rt[q], w_r] + ib, [z_r])
                        dve(lambda e: e.tensor_tensor_scan(z_i[:, 0:n], Rt[q][:, 0:n], w_i[:, 0:n], i0i, ALU.mult, ALU.add),
                            [Rt[q], w_i] + ib, [z_i])
                        if n == TC:
                            dve(lambda e: e.tensor_tensor(em[:, 3:4], z_i[:, TC - 1:TC], ETC[q][:, 1:2], ALU.mult), [z_i, ETC[q]], [em])
                            dve(lambda e: e.scalar_tensor_tensor(car[gp][:, 0:1], z_r[:, TC - 1:TC], ETC[q][:, 0:1], em[:, 3:4],
                                                                 ALU.mult, ALU.subtract), [z_r, ETC[q], em], [car[gp]])
                            dve(lambda e: e.tensor_tensor(em[:, 3:4], z_i[:, TC - 1:TC], ETC[q][:, 0:1], ALU.mult), [z_i, ETC[q]], [em])
                            dve(lambda e: e.scalar_tensor_tensor(car[gp][:, 1:2], z_r[:, TC - 1:TC], ETC[q][:, 1:2], em[:, 3:4],
                                                                 ALU.mult, ALU.add), [z_r, ETC[q], em], [car[gp]])
                        p1, p2, x_r, x_i = pa.next(), pb.next(), xr.next(), xi.next()
                        pool = lambda fn, r, w: kb.op("pool", fn, r, w)
                        pool(lambda e: e.tensor_tensor(p1[:, 0:n], z_r[:, 0:n], Ec[q][:, 0:n], ALU.mult), [z_r, Ec[q]], [p1])
                        pool(lambda e: e.tensor_tensor(p2[:, 0:n], z_i[:, 0:n], Es[q][:, 0:n], ALU.mult), [z_i, Es[q]], [p2])
                        pool(lambda e: e.tensor_tensor(x_r[:, 0:n], p1[:, 0:n], p2[:, 0:n], ALU.subtract), [p1, p2], [x_r])
                        d1, d2 = ta.next(), tb.next()
                        dve(lambda e: e.tensor_tensor(d1[:, 0:n], z_r[:, 0:n], nEs[q][:, 0:n], ALU.mult), [z_r, nEs[q]], [d1])
                        dve(lambda e: e.tensor_tensor(d2[:, 0:n], z_i[:, 0:n], Ec[q][:, 0:n], ALU.mult), [z_i, Ec[q]], [d2])
                        pool(lambda e: e.tensor_tensor(x_i[:, 0:n], d1[:, 0:n], d2[:, 0:n], ALU.subtract), [d1, d2], [x_i])
                        if "S5X" in self.dbg and cc == 0 and d == 0 and ci == 0 and gp == 0:
                            dbx = kb.dram("S5X", [128, 6, TC], F32, "ExternalOutput")
                            xf = sb([128, 2, TC], name="xf")
                            dve(lambda e: e.tensor_copy(xf[:, 0, 0:n], x_r[:, 0:n]), [x_r], [xf])
                            dve(lambda e: e.tensor_copy(xf[:, 1, 0:n], x_i[:, 0:n]), [x_i], [xf])
                            for ki_, tb_ in enumerate((w_r, w_i, z_r, z_i)):
                                kb.dma("sp", dbx, dbx.t[:, ki_, 0:n], tb_, tb_[:, 0:n], nowaw=True)
                            kb.dma("sp", dbx, dbx.t[:, 4:6, 0:n], xf, xf[:, :, 0:n], nowaw=True)
                        xs.append((x_r, x_i, q))
                    py = kb.ps()
                    ys = yst.next()
                    for tt in range(ntt):
                        sl = slice(tt * 128, (tt + 1) * 128)
                        for gi_, (x_r, x_i, q) in enumerate(xs):
                            kb.op("pe", lambda e: e.matmul(py[:, sl], x_r[:, sl], CMs[q][:, 0, :], start=(gi_ == 0), stop=False),
                                  [x_r, CMs[q]], [py])
                            kb.op("pe", lambda e: e.matmul(py[:, sl], x_i[:, sl], CMs[q][:, 1, :], start=False,
                                                           stop=(gi_ == 3 and d == 1)), [x_i, CMs[q]], [py])
                        if d == 1:
                            r = c0 // 128 + tt
                            self.copy_evac(yR[:, r, :], py[:, sl], [py], [yR])
                        else:
                            ti = c0 // 128 + tt
                            r = (1 - ti) if ti < 2 else (35 - ti)
                            kb.op("pe", lambda e: e.matmul(py[:, sl], self.jmat[:], yR[:, r, :], start=False, stop=False),
                                  [self.jmat, yR], [py])
                            kb.op("pe", lambda e: e.matmul(py[:, sl], useq[0][:, ti * 128:(ti + 1) * 128], DG[:], start=False,
                                                           stop=True), [useq[0], DG], [py])
                            self.copy_evac(ys[:, tt, :], py[:, sl], [py], [ys])
                    if d == 0:
                        kb.dma("sp", self.YA, self.YA.t[c0:c0 + n, cc * 128:(cc + 1) * 128].rearrange("(tt p) c -> p tt c", p=128),
                               ys, ys[:, 0:ntt, :], nowaw=True)
        kb.end_phase()


Layer.p_s5 = _layer_s5


def _layer_attn(self):
    kb = self.kb
    dl_d = self.din("dlam", [128, 4, 64])
    gd_d = self.din("gdiff", [128, 128])
    li_d = self.din("lam_init", [128, 2])
    self.ZB = self.scratch("ZB", [8, 128, T])
    self.ZC = self.scratch("ZC", [16, 128, T])
    small = self.phases is not None and "attn_small" in self.phases
    qv = self.Q.t.rearrange("c p t -> p c t")
    with ExitStack() as ph:
        sb = lambda shape, dt=F32, name="at": kb.sbuf(shape, dt, name, ph)
        dve = lambda fn, r, w: kb.op("dve", fn, r, w)
        act = lambda fn, r, w: kb.op("act", fn, r, w)
        dl = sb([128, 4, 64]); gdw = sb([128, 128]); li = sb([128, 2]); lt = sb([128, 64]); lv = sb([128, 4])
        kb.dma("sp", dl, dl[:], dl_d, dl_d[:, :, :])
        kb.dma("sp", gdw, gdw[:], gd_d, gd_d[:, :])
        kb.dma("sp", li, li[:], li_d, li_d[:, :])
        for i in range(2):
            dve(lambda e: e.tensor_tensor(lt[:], dl[:, 2 * i, :], dl[:, 2 * i + 1, :], ALU.mult), [dl], [lt])
            dve(lambda e: e.reduce_sum(lv[:, i:i + 1], lt[:], axis=AX.X), [lt], [lv])
        act(lambda e: e.activation(out=lv[:, 0:2], in_=lv[:, 0:2], func=AF.Exp), [], [lv])
        dve(lambda e: e.tensor_tensor(lv[:, 2:3], lv[:, 1:2], lv[:, 0:1], ALU.subtract), [], [lv])
        dve(lambda e: e.tensor_tensor(lv[:, 3:4], lv[:, 2:3], li[:, 0:1], ALU.subtract), [li], [lv])
        dve(lambda e: e.tensor_scalar(gdw[:], gdw[:], li[:, 1:2], None, ALU.mult), [li], [gdw])
        neg_lam = lv[:, 3:4]
        PT = [sb([128, 34, 512], BF16, "PT") for _ in range(2)]
        kring = Ring(kb, [128, T], BF16, 2, ph, "kT")
        vring = Ring(kb, [128, 34, 129], BF16, 2, ph, "v")
        kr = sb([64, T], BF16, "kr")
        kb.dma("sp", kr, kr[:], self.KR, self.KR.t[:, :])
        qring = Ring(kb, [128, 512], BF16, 3, ph, "q")
        qrr = Ring(kb, [64, 512], BF16, 3, ph, "qr")
        gring = Ring(kb, [128, 512], BF16, 3, ph, "g")
        sgr = Ring(kb, [128, 512], F32, 2, ph, "sg")
        orr = Ring(kb, [128, 2, 128], F32, 2, ph, "o")
        obr = Ring(kb, [128, 128], F32, 2, ph, "ob")
        obn = Ring(kb, [128, 128], BF16, 2, ph, "obn")
        junk = sb([128, 128], F32, "junk")
        sc = Ring(kb, [128, 4], F32, 4, ph, "sc")
        zst = Ring(kb, [128, 512], BF16, 2, ph, "zst")
        for v in vring.bufs:
            dve(lambda e: e.memset(v[:, :, 128:129], 1.0), [], [v])

        def head(kind, h):
            nmap = 2 if kind == "diff" else 1
            scale = 64 ** -0.5 if kind == "diff" else 192 ** -0.5
            kT, vt = kring.next(), vring.next()
            if kind == "diff":
                kb.dma("sp", kT, kT[:], self.DK, self.DK.t[h])
                kb.dma("sp", vt, vt[:, :, 0:128], self.DV,
                       self.DV.t[:, h * 128:(h + 1) * 128].rearrange("(kc p) c -> p kc c", p=128))
                gch, zdst = QC_GB + h, self.ZB
            else:
                kb.dma("sp", kT, kT[:], self.KN, self.KN.t[h])
                kb.dma("sp", vt, vt[:, :, 0:128], self.MV,
                       self.MV.t[:, h * 128:(h + 1) * 128].rearrange("(kc p) c -> p kc c", p=128))
                gch, zdst = QC_GC + h, self.ZC
            for ti, (t0, n, _) in enumerate(TGS):
                if small and ti > 1:
                    break
                nk = 2 if ti == 0 else 34
                qb, gb = qring.next(), gring.next()
                if kind == "diff":
                    kb.dma("sp", qb, qb[:, 0:n], self.Q, self.Q.t[QC_DQ + h, :, t0:t0 + n])
                else:
                    qr_ = qrr.next()
                    kb.dma("sp", qb, qb[:, 0:n], self.QN, self.QN.t[h, :, t0:t0 + n])
                    kb.dma("sp", qr_, qr_[:, 0:n], self.QR, self.QR.t[h, :, t0:t0 + n])
                kb.dma("sp", gb, gb[:, 0:n], self.Q, self.Q.t[gch, :, t0:t0 + n])
                sg = sgr.next()
                act(lambda e: e.activation(out=sg[:, 0:n], in_=gb[:, 0:n], func=AF.Silu), [gb], [sg])
                for m in range(nmap):
                    pt = PT[m] if kind == "diff" else PT[ti % 2]
                    for kc in range(nk):
                        ps = kb.ps()
                        ksl = slice(kc * 128, (kc + 1) * 128)
                        if kind == "diff":
                            kb.op("pe", lambda e: e.matmul(ps[:, 0:n], kT[m * 64:(m + 1) * 64, ksl], qb[m * 64:(m + 1) * 64, 0:n],
                                                           start=True, stop=True), [kT, qb], [ps])
                        else:
                            kb.op("pe", lambda e: e.matmul(ps[:, 0:n], kT[:, ksl], qb[:, 0:n], start=True, stop=False),
                                  [kT, qb], [ps])
                            kb.op("pe", lambda e: e.matmul(ps[:, 0:n], kr[:, ksl], qr_[:, 0:n], start=False, stop=True),
                                  [kr, qr_], [ps])
                        act(lambda e: e.activation(out=pt[:, kc, 0:n], in_=ps[:, 0:n], func=AF.Exp, scale=scale), [ps], [pt])
                zs = zst.next()
                for qt in range(n // 128):
                    qsl = slice(qt * 128, (qt + 1) * 128)
                    o = orr.next()
                    s4 = sc.next()
                    for m in range(nmap):
                        pt = PT[m] if kind == "diff" else PT[ti % 2]
                        po = kb.ps()
                        for kc in range(nk):
                            kb.op("pe", lambda e: e.matmul(po[:, 0:129], pt[:, kc, qsl], vt[:, kc, :], start=(kc == 0),
                                                           stop=(kc == nk - 1)), [pt, vt], [po])
                        dve(lambda e: e.reciprocal(s4[:, m:m + 1], po[:, 128:129]), [po], [s4])
                        dve(lambda e: e.tensor_scalar(o[:, m, :], po[:, 0:128], s4[:, m:m + 1], None, ALU.mult), [po, s4], [o])
                    on = obn.next()
                    if kind == "diff":
                        ob = obr.next()
                        dve(lambda e: e.scalar_tensor_tensor(ob[:], o[:, 1, :], neg_lam, o[:, 0, :], ALU.mult, ALU.add),
                            [o, lv], [ob])
                        act(lambda e: e.activation(out=junk[:], in_=ob[:], func=AF.Square, accum_out=s4[:, 2:3]), [ob], [junk, s4])
                        act(lambda e: e.activation(out=s4[:, 3:4], in_=s4[:, 2:3], func=AF.Sqrt, scale=1.0 / 128, bias=self.eps[:]),
                            [self.eps], [s4])
                        dve(lambda e: e.reciprocal(s4[:, 3:4], s4[:, 3:4]), [], [s4])
                        dve(lambda e: e.scalar_tensor_tensor(on[:], ob[:], s4[:, 3:4], gdw[:], ALU.mult, ALU.mult),
                            [ob, s4, gdw], [on])
                    else:
                        dve(lambda e: e.tensor_copy(on[:], o[:, 0, :]), [o], [on])
                    pz = kb.ps()
                    kb.op("pe", lambda e: e.matmul(pz[:, 0:128], on[:], self.ident[:], start=True, stop=True), [on, self.ident], [pz])
                    dve(lambda e: e.tensor_tensor(zs[:, qsl], pz[:, 0:128], sg[:, qsl], ALU.mult), [pz, sg], [zs])
                kb.dma("sp", zdst, zdst.t[h, :, t0:t0 + n], zs, zs[:, 0:n], nowaw=True)

        heads = [("diff", h) for h in range(8)] + [("mla", h) for h in range(16)]
        if small:
            heads = [("diff", 0), ("mla", 0)]
        for kind, h in heads:
            head(kind, h)
        kb.end_phase()


Layer.p_attn = _layer_attn


def _layer_s5fin(self):
    kb = self.kb
    wglu_d = self.din("w_glu", [1024, 1024])
    self.ZA = self.scratch("ZA", [8, 128, T])
    G1 = 2.0 * math.sqrt(2.0 / math.pi)
    with ExitStack() as ph:
        sb = lambda shape, dt=F32, name="sf": kb.sbuf(shape, dt, name, ph)
        dve = lambda fn, r, w: kb.op("dve", fn, r, w)
        act = lambda fn, r, w: kb.op("act", fn, r, w)
        wglu = sb([128, 8, 1024], BF16, "wglu")
        kb.dma("pool", wglu, wglu[:], wglu_d, wglu_d.t.rearrange("(kc p) n -> p kc n", p=128))
        yar = Ring(kb, [128, 4, 1024], BF16, 2, ph, "ya")
        gar = Ring(kb, [128, 8, 512], BF16, 2, ph, "ga")
        zg = sb([128, 8, 512], BF16, "zg")
        xs, x2, vv, sg = sb([128, 512]), sb([128, 512]), sb([128, 512]), sb([128, 512])
        sgl, sga, tt_ = sb([128, 512]), sb([128, 512]), sb([128, 512])
        zst = Ring(kb, [128, 8, 512], BF16, 2, ph, "zst")
        qv = self.Q.t.rearrange("c p t -> p c t")
        bufs = {}

        def load(i):
            t0, n, _ = TGS[i]
            a, g = yar.next(), gar.next()
            kb.dma("sp", a, a[:, 0:n // 128, :], self.YA, self.YA.t[t0:t0 + n, :].rearrange("(tt p) c -> p tt c", p=128))
            kb.dma("sp", g, g[:, :, 0:n], self.Q, qv[:, QC_GA:QC_GA + 8, t0:t0 + n])
            bufs[i] = (a, g)
        load(0)
        for i, (t0, n, _) in enumerate(TGS):
            if i + 1 < len(TGS):
                load(i + 1)
            ya, ga = bufs.pop(i)
            for c in range(8):
                ps = kb.ps()
                for tt in range(n // 128):
                    kb.op("pe", lambda e: e.matmul(ps[:, tt * 128:(tt + 1) * 128], ya[:, tt, c * 128:(c + 1) * 128], self.ident[:],
                                                   start=True, stop=True), [ya, self.ident], [ps])
                dve(lambda e: e.tensor_copy(xs[:, 0:n], ps[:, 0:n]), [ps], [xs])
                act(lambda e: e.activation(out=x2[:, 0:n], in_=xs[:, 0:n], func=AF.Square), [xs], [x2])
                dve(lambda e: e.tensor_scalar(x2[:, 0:n], x2[:, 0:n], 0.044715, 1.0, ALU.mult, ALU.add), [], [x2])
                dve(lambda e: e.tensor_tensor(vv[:, 0:n], x2[:, 0:n], xs[:, 0:n], ALU.mult), [x2, xs], [vv])
                act(lambda e: e.activation(out=sg[:, 0:n], in_=vv[:, 0:n], func=AF.Sigmoid, scale=G1), [vv], [sg])
                dve(lambda e: e.tensor_tensor(zg[:, c, 0:n], xs[:, 0:n], sg[:, 0:n], ALU.mult), [xs, sg], [zg])
            zs = zst.next()
            for c2 in range(8):
                ps = kb.ps()
                for c in range(8):
                    kb.op("pe", lambda e: e.matmul(ps[:, 0:n], wglu[:, c, c2 * 128:(c2 + 1) * 128], zg[:, c, 0:n],
                                                   start=(c == 0), stop=(c == 7)), [wglu, zg], [ps])
                act(lambda e: e.activation(out=sgl[:, 0:n], in_=ps[:, 0:n], func=AF.Sigmoid), [ps], [sgl])
                act(lambda e: e.activation(out=sga[:, 0:n], in_=ga[:, c2, 0:n], func=AF.Silu), [ga], [sga])
                dve(lambda e: e.tensor_tensor(tt_[:, 0:n], zg[:, c2, 0:n], sgl[:, 0:n], ALU.mult), [zg, sgl], [tt_])
                dve(lambda e: e.tensor_tensor(zs[:, c2, 0:n], tt_[:, 0:n], sga[:, 0:n], ALU.mult), [tt_, sga], [zs])
            kb.dma("sp", self.ZA, self.ZA.t.rearrange("c p t -> p c t")[:, :, t0:t0 + n], zs, zs[:, :, 0:n], nowaw=True)
        kb.end_phase()


def _layer_merge(self):
    kb = self.kb
    wa_d = self.din("w_br_s5", [1024, D])
    wb_d = self.din("w_br_diff", [1024, D])
    wc_d = self.din("w_br_mla", [2048, D])
    self.Y = self.scratch("Y", [32, 128, T])
    with ExitStack() as ph:
        sb = lambda shape, dt=F32, name="mg": kb.sbuf(shape, dt, name, ph)
        dve = lambda fn, r, w: kb.op("dve", fn, r, w)
        act = lambda fn, r, w: kb.op("act", fn, r, w)
        wring = Ring(kb, [128, 32, 512], BF16, 2, ph, "w")
        aring = Ring(kb, [128, 32, 512], BF16, 2, ph, "a")
        gring = Ring(kb, [128, 3, 4, 512], BF16, 2, ph, "gm")
        sgr = Ring(kb, [128, 3, 512], F32, 2, ph, "sig")
        t1, t2 = sb([128, 512]), sb([128, 512])
        stg = Ring(kb, [128, 4, 512], BF16, 2, ph, "stg")
        qv = self.Q.t.rearrange("c p t -> p c t")
        zav, zbv, zcv = (z.t.rearrange("c p t -> p c t") for z in (self.ZA, self.ZB, self.ZC))
        steps = [(cg, ti) for cg in range(8) for ti in range(len(TGS))]
        wb_, ab_ = {}, {}

        def prefetch(s):
            cg, ti = steps[s]
            c0 = cg * 512
            if ti == 0:
                w = wring.next()
                kb.dma("pool", w, w[:, 0:8, :], wa_d, wa_d.t.rearrange("(kc p) n -> p kc n", p=128)[:, :, c0:c0 + 512])
                kb.dma("pool", w, w[:, 8:16, :], wb_d, wb_d.t.rearrange("(kc p) n -> p kc n", p=128)[:, :, c0:c0 + 512])
                kb.dma("pool", w, w[:, 16:32, :], wc_d, wc_d.t.rearrange("(kc p) n -> p kc n", p=128)[:, :, c0:c0 + 512])
                wb_[cg] = w
            t0, n, _ = TGS[ti]
            a, g = aring.next(), gring.next()
            kb.dma("sp", a, a[:, 0:8, 0:n], self.ZA, zav[:, :, t0:t0 + n])
            kb.dma("sp", a, a[:, 8:16, 0:n], self.ZB, zbv[:, :, t0:t0 + n])
            kb.dma("sp", a, a[:, 16:32, 0:n], self.ZC, zcv[:, :, t0:t0 + n])
            for s3 in range(3):
                q0 = QC_GM + s3 * 32 + cg * 4
                kb.dma("sp", g, g[:, s3, :, 0:n], self.Q, qv[:, q0:q0 + 4, t0:t0 + n])
            ab_[s] = (a, g)
        prefetch(0)
        for s, (cg, ti) in enumerate(steps):
            if s + 1 < len(steps):
                prefetch(s + 1)
            t0, n, _ = TGS[ti]
            w = wb_[cg]
            a, g = ab_.pop(s)
            st = stg.next()
            for j in range(4):
                csl = slice(j * 128, (j + 1) * 128)
                pss = []
                for (k0, k1) in ((0, 8), (8, 16), (16, 32)):
                    ps = kb.ps()
                    for kc in range(k0, k1):
                        kb.op("pe", lambda e: e.matmul(ps[:, 0:n], w[:, kc, csl], a[:, kc, 0:n], start=(kc == k0), stop=(kc == k1 - 1)),
                              [w, a], [ps])
                    pss.append(ps)
                sig = sgr.next()
                act(lambda e: e.activation(out=sig[:, :, 0:n], in_=g[:, :, j, 0:n], func=AF.Sigmoid), [g], [sig])
                dve(lambda e: e.tensor_tensor(t1[:, 0:n], pss[0][:, 0:n], sig[:, 0, 0:n], ALU.mult), [pss[0], sig], [t1])
                dve(lambda e: e.tensor_tensor(t2[:, 0:n], pss[1][:, 0:n], sig[:, 1, 0:n], ALU.mult), [pss[1], sig], [t2])
                dve(lambda e: e.tensor_tensor(t1[:, 0:n], t1[:, 0:n], t2[:, 0:n], ALU.add), [t2], [t1])
                dve(lambda e: e.tensor_tensor(t2[:, 0:n], pss[2][:, 0:n], sig[:, 2, 0:n], ALU.mult), [pss[2], sig], [t2])
                dve(lambda e: e.tensor_tensor(st[:, j, 0:n], t1[:, 0:n], t2[:, 0:n], ALU.add), [t1, t2], [st])
            kb.dma("sp", self.Y, self.Y.t.rearrange("c p t -> p c t")[:, cg * 4:cg * 4 + 4, t0:t0 + n], st, st[:, :, 0:n], nowaw=True)
        kb.end_phase()


def _layer_out(self):
    kb = self.kb
    wo_d = self.din("w_out", [D, D])
    self.OUT = self.scratch("OUT", [32, 128, T], F32)
    if self.last:
        self.xT_out = kb.dram("xT_out", [D, T], F32, "ExternalOutput")
    else:
        self.xT_out = self.scratch("XS%d" % (self.lidx % 2), [D, T], F32)
    with ExitStack() as ph:
        wring = Ring(kb, [128, 32, 512], BF16, 2, ph, "w")
        aring = Ring(kb, [128, 32, 512], BF16, 2, ph, "a")
        stg = Ring(kb, [128, 4, 512], F32, 2, ph, "stg")
        yv = self.Y.t.rearrange("c p t -> p c t")
        wv = wo_d.t.rearrange("(kc p) n -> p kc n", p=128)
        steps = [(cg, ti) for cg in range(8) for ti in range(len(TGS))]
        wb_, ab_ = {}, {}

        def prefetch(s):
            cg, ti = steps[s]
            if ti == 0:
                w = wring.next()
                kb.dma("pool", w, w[:], wo_d, wv[:, :, cg * 512:(cg + 1) * 512])
                wb_[cg] = w
            t0, n, _ = TGS[ti]
            a = aring.next()
            kb.dma("sp", a, a[:, :, 0:n], self.Y, yv[:, :, t0:t0 + n])
            ab_[s] = a
        prefetch(0)
        for s, (cg, ti) in enumerate(steps):
            if s + 1 < len(steps):
                prefetch(s + 1)
            t0, n, _ = TGS[ti]
            w, a = wb_[cg], ab_.pop(s)
            pss = [kb.ps() for _ in range(4)]
            for kc in range(32):
                for j in range(4):
                    kb.op("pe", lambda e: e.matmul(pss[j][:, 0:n], w[:, kc, j * 128:(j + 1) * 128], a[:, kc, 0:n],
                                                   start=(kc == 0), stop=(kc == 31)), [w, a], [pss[j]])
            st = stg.next()
            for j in range(4):
                self.copy_evac(st[:, j, 0:n], pss[j][:, 0:n], [pss[j]], [st])
            kb.dma("sp", self.OUT, self.OUT.t.rearrange("c p t -> p c t")[:, cg * 4:cg * 4 + 4, t0:t0 + n], st, st[:, :, 0:n],
                   nowaw=True)
        kb.end_phase()
    with ExitStack() as ph:
        sb = lambda shape, dt=F32, name="fn": kb.sbuf(shape, dt, name, ph)
        dve = lambda fn, r, w: kb.op("dve", fn, r, w)
        act = lambda fn, r, w: kb.op("act", fn, r, w)
        ob = Ring(kb, [128, 32, 512], F32, 2, ph, "ob")
        xr = Ring(kb, [128, 512], F32, 4, ph, "xr")
        sqr = Ring(kb, [128, 512], BF16, 3, ph, "sq")
        tr = Ring(kb, [128, 512], F32, 3, ph, "t")
        xo = Ring(kb, [128, 512], F32, 3, ph, "xo")
        rstd = sb([128, 512])
        ov = self.OUT.t.rearrange("c p t -> p c t")
        xv = self.xT.t.rearrange("(kc p) t -> p kc t", p=128)
        xov = self.xT_out.t.rearrange("(kc p) t -> p kc t", p=128)
        bufs = {}

        def load(i):
            t0, n, _ = TGS[i]
            b = ob.next()
            kb.dma("sp", b, b[:, :, 0:n], self.OUT, ov[:, :, t0:t0 + n])
            bufs[i] = b
        load(0)
        for i, (t0, n, v) in enumerate(TGS):
            if i + 1 < len(TGS):
                load(i + 1)
            o = bufs.pop(i)
            ss = kb.ps()
            for kc in range(32):
                sq = sqr.next()
                act(lambda e: e.activation(out=sq[:, 0:n], in_=o[:, kc, 0:n], func=AF.Square), [o], [sq])
                kb.op("pe", lambda e: e.matmul(ss[:, 0:n], self.ones[:], sq[:, 0:n], start=(kc == 0), stop=(kc == 31)),
                      [self.ones, sq], [ss])
            act(lambda e: e.activation(out=rstd[:, 0:n], in_=ss[:, 0:n], func=AF.Sqrt, scale=1.0 / D, bias=self.eps[:]),
                [ss, self.eps], [rstd])
            dve(lambda e: e.reciprocal(rstd[:, 0:n], rstd[:, 0:n]), [], [rstd])
            for kc in range(32):
                x = xr.next()
                kb.dma("sp", x, x[:, 0:n], self.xT, xv[:, kc, t0:t0 + n])
                t, y = tr.next(), xo.next()
                dve(lambda e: e.scalar_tensor_tensor(t[:, 0:n], o[:, kc, 0:n], self.gpg_s[:, kc, v:v + 1], rstd[:, 0:n],
                                                     ALU.mult, ALU.mult), [o, self.gpg_s, rstd], [t])
                kb.op("pool", lambda e: e.tensor_tensor(y[:, 0:n], t[:, 0:n], x[:, 0:n], ALU.add), [t, x], [y])
                kb.dma("sp", self.xT_out, xov[:, kc, t0:t0 + n], y, y[:, 0:n], nowaw=True)
        kb.end_phase()


Layer.p_s5fin = _layer_s5fin
Layer.p_merge = _layer_merge
Layer.p_out = _layer_out


def build_program(nlayers=DEPTH, dbg=()):
    L = None
    for l in range(nlayers):
        L = Layer(dbg=dbg, prev=L, lidx=l, last=(l == nlayers - 1))
        L.consts()
        L.p_mod()
        L.p_prenorm()
        L.p_proj()
        L.p_mla_proj()
        L.p_s5()
        L.p_s5fin()
        L.p_attn()
        L.p_merge()
        L.p_out()
    return L.kb.finish()


LAYER_KEYS = ("w_mod", "w_in", "w_uq", "w_ukv", "w_glu", "w_br_s5", "w_br_diff", "w_br_mla", "w_out")


def layer_inputs(inp, l, b, consts, single):
    sfx = "" if single else "_L%d" % l
    im = {}
    for k in LAYER_KEYS:
        im[k + sfx] = inp[k][l]
    im["bmod" + sfx] = pp_layout(inp["b_mod"][l])
    im["gpre" + sfx] = pp_layout(inp["g_pre"][l])
    im["gpost" + sfx] = pp_layout(inp["g_post"][l])
    im["gq" + sfx] = pp_layout(inp["g_q"][l])
    im["gkv" + sfx] = pp_layout(inp["g_kv"][l])
    im["dlam" + sfx] = np.ascontiguousarray(np.broadcast_to(inp["diff_lambda"][l][None], (128, 4, 64)))
    im["gdiff" + sfx] = np.ascontiguousarray(np.broadcast_to(inp["g_diff"][l][None], (128, 128)))
    lam_init = 0.8 - 0.6 * math.exp(-0.3 * l)
    im["lam_init" + sfx] = np.ascontiguousarray(np.broadcast_to(np.array([lam_init, 1.0 - lam_init], np.float32)[None], (128, 2)))
    for k, v in host_s5(inp, l).items():
        im[k + sfx] = v
    return im


def kernel(**inputs):
    inp = {k: np.asarray(v) for k, v in inputs.items()}
    consts = host_consts()
    nc = build_program(DEPTH)
    in_maps = []
    for b in range(NB):
        im = dict(consts)
        im["xT"] = np.ascontiguousarray(np.concatenate([inp["ctx"][b], inp["x"][b]], axis=0).T.astype(np.float32))
        cv = np.stack([inp["c"][b], inp["c_ctx"]], axis=-1)
        im["cvec"] = np.ascontiguousarray(cv.reshape(32, 128, 2).transpose(1, 0, 2))
        for l in range(DEPTH):
            im.update(layer_inputs(inp, l, b, consts, False))
        in_maps.append(im)
    res = run_spmd(nc, in_maps)
    out = np.stack([res[b]["xT_out"].T[LC:] for b in range(NB)], axis=0)
    return np.ascontiguousarray(out.astype(np.float32))
```
